# Optimizing a Trainium2 kernel written in Bass

```python
import math
import jax
import jax.numpy as jnp
from jax import lax
import numpy as np

D_MODEL = 4096
BATCH = 4
SEQ = 2048
DEPTH = 1
DEC_BATCH = 32
DEC_SEQ = 4
PAST_LEN = 8192
PAGE_SIZE = 128

HEAD_DIM = 128
GDN_HEADS = D_MODEL // (2 * HEAD_DIM)
NSA_HEADS = D_MODEL // (2 * HEAD_DIM)
NSA_KV_HEADS = 4
NSA_GROUP = NSA_HEADS // NSA_KV_HEADS
GDN_WIDTH = GDN_HEADS * HEAD_DIM
NSA_WIDTH = NSA_HEADS * HEAD_DIM
MIX_WIDTH = GDN_WIDTH + NSA_WIDTH
KV_WIDTH = 2 * NSA_KV_HEADS * HEAD_DIM
IN_DIM = 4 * GDN_WIDTH + 2 * GDN_HEADS + NSA_WIDTH + 3 * KV_WIDTH + 3 * NSA_HEADS
CONV_W = 4
GDN_CHUNK = 64
CMP_BLOCK = 32
CMP_STRIDE = 16
SLC_BLOCK = 64
SLC_TOPK = 16
SLC_LOCAL = 2
WINDOW = 512
WIN_QBLOCK = 128
SLC_QBLOCK = 32
D_FF = 4 * D_MODEL
EPS = 1e-6
FORCE_SCORE = 1e9

kernel_name = 'gdn_nsa_parallel_hybrid_step'


def rms_norm(x, g):
    xf = x.astype(jnp.float32)
    y = xf * lax.rsqrt(jnp.mean(xf * xf, axis=-1, keepdims=True) + EPS)
    return (y * g.astype(jnp.float32)).astype(x.dtype)


def l2_norm(x):
    return x * lax.rsqrt(jnp.sum(x * x, axis=-1, keepdims=True) + EPS)


def masked_softmax(s, mask):
    s = jnp.where(mask, s, -jnp.inf)
    m = jnp.max(s, axis=-1, keepdims=True)
    m = jnp.where(jnp.isfinite(m), m, 0.0)
    p = jnp.exp(s - m)
    return p / jnp.maximum(jnp.sum(p, axis=-1, keepdims=True), jnp.finfo(jnp.float32).tiny)


def _in_splits():
    sizes = [GDN_WIDTH] * 4 + [GDN_HEADS] * 2 + [NSA_WIDTH, KV_WIDTH, KV_WIDTH, KV_WIDTH, 3 * NSA_HEADS]
    return [int(s) for s in np.cumsum(sizes)[:-1]]


def short_conv(x, buf, w):
    t = x.shape[1]
    xp = jnp.concatenate([buf.astype(x.dtype), x], axis=1)
    y = xp[:, 0:t] * w[0]
    for j in range(1, CONV_W):
        y = y + xp[:, j:j + t] * w[j]
    return jax.nn.silu(y), xp[:, t:]


def _to_chunks(a, pad, nc):
    b = a.shape[0]
    a = jnp.pad(a, [(0, 0), (0, pad)] + [(0, 0)] * (a.ndim - 2))
    a = a.reshape((b, nc, GDN_CHUNK) + a.shape[2:])
    return jnp.moveaxis(a, (1, 3), (0, 2))


def gated_delta_rule(q, k, v, g, beta, s0):
    b, t, h, _ = q.shape
    dv = v.shape[-1]
    pad = (-t) % GDN_CHUNK
    nc = (t + pad) // GDN_CHUNK
    qc, kc, vc, gc, bc = [_to_chunks(a, pad, nc) for a in (q, k, v, g, beta)]
    gcum = jnp.cumsum(gc, axis=-1)
    tri = np.tril(np.ones((GDN_CHUNK, GDN_CHUNK), dtype=bool))
    strict = np.tril(np.ones((GDN_CHUNK, GDN_CHUNK), dtype=bool), -1)
    decay = jnp.exp(jnp.where(tri, gcum[..., :, None] - gcum[..., None, :], -jnp.inf))
    kb = kc * bc[..., None]
    lmat = jnp.eye(GDN_CHUNK, dtype=q.dtype) + jnp.where(
        strict, jnp.einsum('nbhid,nbhjd->nbhij', kb, kc) * decay, 0.0)
    u = lax.linalg.triangular_solve(lmat, vc * bc[..., None], left_side=True, lower=True, unit_diagonal=True)
    w = lax.linalg.triangular_solve(lmat, kb * jnp.exp(gcum)[..., None], left_side=True, lower=True,
                                    unit_diagonal=True)
    a_qk = jnp.einsum('nbhid,nbhjd->nbhij', qc, kc) * decay

    def step(s, xs):
        q_i, k_i, u_i, w_i, g_i, a_i = xs
        v_new = u_i - jnp.einsum('bhck,bhkv->bhcv', w_i, s)
        o_i = (jnp.einsum('bhck,bhkv->bhcv', q_i * jnp.exp(g_i)[..., None], s)
               + jnp.einsum('bhij,bhjv->bhiv', a_i, v_new))
        g_last = g_i[..., -1]
        k_dec = k_i * jnp.exp(g_last[..., None] - g_i)[..., None]
        s = s * jnp.exp(g_last)[..., None, None] + jnp.einsum('bhck,bhcv->bhkv', k_dec, v_new)
        return s, o_i

    s_fin, o = lax.scan(step, s0, (qc, kc, u, w, gcum, a_qk))
    o = jnp.moveaxis(o, (0, 2), (1, 3)).reshape(b, nc * GDN_CHUNK, h, dv)[:, :t]
    return o, s_fin


def gdn_mixer(q_raw, k_raw, v_raw, z, b_raw, a_raw, conv_buf, s0, conv_w, a_log, dt_bias, norm_g):
    bsz, t, _ = q_raw.shape
    f32 = jnp.float32
    qkv, conv_new = short_conv(jnp.concatenate([q_raw, k_raw, v_raw], axis=-1), conv_buf, conv_w)
    qkv = qkv.astype(f32).reshape(bsz, t, 3, GDN_HEADS, HEAD_DIM)
    q = l2_norm(qkv[:, :, 0]) * HEAD_DIM ** -0.5
    k = l2_norm(qkv[:, :, 1])
    v = qkv[:, :, 2]
    beta = jax.nn.sigmoid(b_raw.astype(f32))
    g = -jnp.exp(a_log.astype(f32)) * jax.nn.softplus(a_raw.astype(f32) + dt_bias.astype(f32))
    o, s_new = gated_delta_rule(q, k, v, g, beta, s0.astype(f32))
    o = rms_norm(o, norm_g) * jax.nn.silu(z.astype(f32).reshape(bsz, t, GDN_HEADS, HEAD_DIM))
    return o.reshape(bsz, t, GDN_WIDTH).astype(q_raw.dtype), conv_new, s_new.astype(s0.dtype)


def compress_blocks(rows, pos_w, phi):
    bsz, length, grp, d = rows.shape
    n_cmp = (length - CMP_BLOCK) // CMP_STRIDE + 1
    pad = (-length) % CMP_STRIDE
    sub = jnp.pad(rows, ((0, 0), (0, pad), (0, 0), (0, 0))).reshape(
        bsz, (length + pad) // CMP_STRIDE, CMP_STRIDE, grp, d)
    parts = [jnp.einsum('bmjgd,jg->bmgd', sub, pos_w[r * CMP_STRIDE:(r + 1) * CMP_STRIDE])[:, r:r + n_cmp]
             for r in range(CMP_BLOCK // CMP_STRIDE)]
    pooled = parts[0]
    for p in parts[1:]:
        pooled = pooled + p
    return jnp.einsum('bngd,gde->bnge', pooled, phi)


def _slc_overlap(n_cmp, n_slc):
    cs = np.arange(n_cmp) * CMP_STRIDE
    ss = np.arange(n_slc) * SLC_BLOCK
    lo = np.maximum(cs[:, None], ss[None, :])
    hi = np.minimum(cs[:, None] + CMP_BLOCK, ss[None, :] + SLC_BLOCK)
    return (np.maximum(hi - lo, 0) / CMP_BLOCK).astype(np.float32)


def selected_attention(q, kv_all, sel_idx, qpos):
    bsz, t, grp, rep, d = q.shape
    length = kv_all.shape[1]
    n_slc = -(-length // SLC_BLOCK)
    n_sel = sel_idx.shape[-1]
    kvp = jnp.pad(kv_all, ((0, 0), (0, n_slc * SLC_BLOCK - length), (0, 0), (0, 0), (0, 0)))
    kvb = jnp.moveaxis(kvp.reshape(bsz, n_slc, SLC_BLOCK, 2, grp, d), 4, 1)
    qb = math.gcd(t, SLC_QBLOCK)
    nb = t // qb
    qs = jnp.moveaxis(q.reshape(bsz, nb, qb, grp, rep, d), 1, 0)
    ids = jnp.moveaxis(sel_idx.reshape(bsz, grp, nb, qb, n_sel), 2, 0)
    pos = jnp.asarray(qpos.reshape(nb, qb), dtype=jnp.int32)
    bi = jnp.arange(bsz)[:, None, None, None]
    gi = jnp.arange(grp)[None, :, None, None]
    scale = d ** -0.5

    def block(args):
        qq, ii, pp = args
        sel = kvb[bi, gi, ii]
        kpos = ii[..., None] * SLC_BLOCK + jnp.arange(SLC_BLOCK)
        mask = (kpos <= pp[None, None, :, None, None]).reshape(bsz, grp, 1, qb, n_sel * SLC_BLOCK)
        s = jnp.einsum('bqgrd,bgqksd->bgrqks', qq, sel[..., 0, :]).reshape(
            bsz, grp, rep, qb, n_sel * SLC_BLOCK).astype(jnp.float32) * scale
        p = masked_softmax(s, mask).reshape(bsz, grp, rep, qb, n_sel, SLC_BLOCK).astype(qq.dtype)
        return jnp.einsum('bgrqks,bgqksd->bqgrd', p, sel[..., 1, :])

    o = lax.map(block, (qs, ids, pos))
    return jnp.moveaxis(o, 0, 1).reshape(bsz, t, grp, rep, d)


def window_attention(q, kv_all, n_prev):
    bsz, t, grp, rep, d = q.shape
    qb = math.gcd(t, WIN_QBLOCK)
    nb = t // qb
    span = qb + WINDOW
    kv_pad = jnp.pad(kv_all, ((0, 0), (WINDOW, 0), (0, 0), (0, 0), (0, 0)))
    idx = n_prev + np.arange(nb)[:, None] * qb + np.arange(span)[None, :]
    kw = kv_pad[:, idx]
    krel = np.arange(nb)[:, None] * qb - WINDOW + np.arange(span)[None, :]
    qrel = np.arange(nb)[:, None] * qb + np.arange(qb)[None, :]
    diff = qrel[:, :, None] - krel[:, None, :]
    mask = (diff >= 0) & (diff < WINDOW) & (krel[:, None, :] >= -n_prev)
    qq = q.reshape(bsz, nb, qb, grp, rep, d)
    s = jnp.einsum('bnqgrd,bnsgd->bngrqs', qq, kw[:, :, :, 0]).astype(jnp.float32) * d ** -0.5
    p = masked_softmax(s, mask[None, :, None, None]).astype(q.dtype)
    o = jnp.einsum('bngrqs,bnsgd->bnqgrd', p, kw[:, :, :, 1])
    return o.reshape(bsz, t, grp, rep, d)


def _norm_keys(kv, g):
    return jnp.stack([rms_norm(kv[:, :, 0], g), kv[:, :, 1]], axis=2)


def nsa_mixer(q_raw, cmp_raw, slc_raw, win_raw, gate_raw, cmp_past, slc_past, win_past,
              q_norm_g, k_norm_g, cmp_pos_w, cmp_phi):
    bsz, t, _ = q_raw.shape
    grp, rep, d = NSA_KV_HEADS, NSA_GROUP, HEAD_DIM
    offset = cmp_past.shape[1]
    n_prev_win = win_past.shape[1]
    kv_shape = (bsz, t, 2, grp, d)
    q = rms_norm(q_raw.reshape(bsz, t, grp, rep, d), q_norm_g)
    cmp_new = cmp_raw.reshape(kv_shape)
    slc_new = _norm_keys(slc_raw.reshape(kv_shape), k_norm_g[1])
    win_new = _norm_keys(win_raw.reshape(kv_shape), k_norm_g[2])
    cmp_all = jnp.concatenate([cmp_past.astype(cmp_new.dtype), cmp_new], axis=1)
    slc_all = jnp.concatenate([slc_past.astype(slc_new.dtype), slc_new], axis=1)
    win_all = jnp.concatenate([win_past.astype(win_new.dtype), win_new], axis=1)
    qpos = offset + np.arange(t)

    kc = rms_norm(compress_blocks(cmp_all[:, :, 0], cmp_pos_w[0], cmp_phi[0]), k_norm_g[0])
    vc = compress_blocks(cmp_all[:, :, 1], cmp_pos_w[1], cmp_phi[1])
    n_cmp = kc.shape[1]
    blk_end = np.arange(n_cmp) * CMP_STRIDE + CMP_BLOCK - 1
    s = jnp.einsum('btgrd,bngd->bgrtn', q, kc).astype(jnp.float32) * d ** -0.5
    p_cmp = masked_softmax(s, blk_end[None, :] <= qpos[:, None])
    o_cmp = jnp.einsum('bgrtn,bngd->btgrd', p_cmp.astype(vc.dtype), vc)

    n_slc = -(-(offset + t) // SLC_BLOCK)
    imp = jnp.einsum('bgrtn,nj->bgtj', p_cmp, jnp.asarray(_slc_overlap(n_cmp, n_slc)))
    cur = qpos // SLC_BLOCK
    blk = np.arange(n_slc)
    forced = (blk[None] == 0) | ((blk[None] <= cur[:, None]) & (blk[None] > cur[:, None] - SLC_LOCAL))
    future = blk[None] > cur[:, None]
    score = jnp.where(forced, FORCE_SCORE, jnp.where(future, -FORCE_SCORE, imp))
    _, sel_idx = lax.top_k(score, min(SLC_TOPK, n_slc))
    o_slc = selected_attention(q, slc_all, sel_idx, qpos)

    o_win = window_attention(q, win_all, n_prev_win)

    gates = jax.nn.sigmoid(gate_raw.astype(jnp.float32)).reshape(bsz, t, grp, rep, 3).astype(q.dtype)
    o = gates[..., 0:1] * o_cmp + gates[..., 1:2] * o_slc + gates[..., 2:3] * o_win
    keep = min(WINDOW, win_all.shape[1])
    win_state = win_all[:, win_all.shape[1] - keep:]
    return o.reshape(bsz, t, NSA_WIDTH).astype(q_raw.dtype), cmp_new, slc_new, win_state


def hybrid_layer(x, cmp_past, slc_past, win_past, conv_buf, s0,
                 attn_norm_g, w_in, gdn_conv_w, gdn_a_log, gdn_dt_bias, gdn_norm_g,
                 q_norm_g, k_norm_g, cmp_pos_w, cmp_phi, w_o, mlp_norm_g, w_up, w_down):
    h = rms_norm(x, attn_norm_g)
    proj = jnp.einsum('btd,de->bte', h, w_in)
    gq, gk, gv, gz, gb, ga, nq, ncmp, nslc, nwin, ngate = jnp.split(proj, _in_splits(), axis=-1)
    o_gdn, conv_new, s_new = gdn_mixer(gq, gk, gv, gz, gb, ga, conv_buf, s0,
                                       gdn_conv_w, gdn_a_log, gdn_dt_bias, gdn_norm_g)
    o_nsa, cmp_new, slc_new, win_new = nsa_mixer(nq, ncmp, nslc, nwin, ngate, cmp_past, slc_past, win_past,
                                                 q_norm_g, k_norm_g, cmp_pos_w, cmp_phi)
    x = x + jnp.einsum('bte,ed->btd', jnp.concatenate([o_gdn, o_nsa], axis=-1), w_o)
    h2 = rms_norm(x, mlp_norm_g)
    hid = jnp.square(jax.nn.relu(jnp.einsum('btd,df->btf', h2, w_up)))
    x = x + jnp.einsum('btf,fd->btd', hid, w_down)
    return x, cmp_new, slc_new, win_new, conv_new, s_new


def setup_inputs(seed: int = 0) -> dict:
    key = jax.random.key(seed)
    ks = jax.random.split(key, 24)
    n_pages = PAST_LEN // PAGE_SIZE
    n_used = DEC_BATCH * n_pages
    n_phys = n_used + (n_used + 3) // 4
    grp, d = NSA_KV_HEADS, HEAD_DIM
    w_buf = min(WINDOW, PAST_LEN)
    f32 = jnp.float32

    def nrm(k, shape, s=1.0):
        return s * jax.random.normal(k, shape, f32)

    def gain(k, shape):
        return 1.0 + 0.02 * jax.random.normal(k, shape, f32)

    dt = jnp.exp(jax.random.uniform(ks[12], (DEPTH, GDN_HEADS), f32, math.log(1e-3), math.log(1e-1)))
    return {
        'x_prompt': nrm(ks[0], (BATCH, SEQ, D_MODEL)),
        'x_sample': nrm(ks[1], (DEC_BATCH, DEC_SEQ, D_MODEL)),
        'cache_cmp_kv': nrm(ks[2], (DEPTH, n_phys, PAGE_SIZE, 2, grp, d)),
        'cache_slc_kv': nrm(ks[3], (DEPTH, n_phys, PAGE_SIZE, 2, grp, d)),
        'cache_win_kv': nrm(ks[4], (DEPTH, DEC_BATCH, w_buf, 2, grp, d)),
        'cache_gdn_conv': nrm(ks[5], (DEPTH, DEC_BATCH, CONV_W - 1, 3 * GDN_WIDTH)),
        'state_gdn': nrm(ks[6], (DEPTH, DEC_BATCH, GDN_HEADS, d, d), 0.1),
        'page_table': jax.random.permutation(ks[7], n_phys)[:n_used].reshape(DEC_BATCH, n_pages).astype(jnp.int32),
        'attn_norm_g': gain(ks[8], (DEPTH, D_MODEL)),
        'w_in': nrm(ks[9], (DEPTH, D_MODEL, IN_DIM), D_MODEL ** -0.5),
        'gdn_conv_w': nrm(ks[10], (DEPTH, CONV_W, 3 * GDN_WIDTH), CONV_W ** -0.5),
        'gdn_a_log': jnp.log(jax.random.uniform(ks[11], (DEPTH, GDN_HEADS), f32, 1.0, 16.0)),
        'gdn_dt_bias': dt + jnp.log(-jnp.expm1(-dt)),
        'gdn_norm_g': gain(ks[13], (DEPTH, HEAD_DIM)),
        'q_norm_g': gain(ks[14], (DEPTH, HEAD_DIM)),
        'k_norm_g': gain(ks[15], (DEPTH, 3, HEAD_DIM)),
        'cmp_pos_w': (1.0 + 0.1 * jax.random.normal(ks[16], (DEPTH, 2, CMP_BLOCK, grp), f32)) / CMP_BLOCK,
        'cmp_phi': nrm(ks[17], (DEPTH, 2, grp, d, d), d ** -0.5),
        'w_o': nrm(ks[18], (DEPTH, MIX_WIDTH, D_MODEL), MIX_WIDTH ** -0.5),
        'mlp_norm_g': gain(ks[19], (DEPTH, D_MODEL)),
        'w_up': nrm(ks[20], (DEPTH, D_MODEL, D_FF), D_MODEL ** -0.5),
        'w_down': nrm(ks[21], (DEPTH, D_FF, D_MODEL), D_FF ** -0.5),
    }


def reference(x_prompt, x_sample, cache_cmp_kv, cache_slc_kv, cache_win_kv, cache_gdn_conv, state_gdn,
              page_table, attn_norm_g, w_in, gdn_conv_w, gdn_a_log, gdn_dt_bias, gdn_norm_g, q_norm_g,
              k_norm_g, cmp_pos_w, cmp_phi, w_o, mlp_norm_g, w_up, w_down):
    bsz = x_prompt.shape[0]
    dbsz = x_sample.shape[0]
    past = page_table.shape[1] * PAGE_SIZE
    kv_tail = (2, NSA_KV_HEADS, HEAD_DIM)
    empty = jnp.zeros((bsz, 0) + kv_tail, x_prompt.dtype)
    conv0 = jnp.zeros((bsz, CONV_W - 1, 3 * GDN_WIDTH), x_prompt.dtype)
    s_zero = jnp.zeros((bsz, GDN_HEADS, HEAD_DIM, HEAD_DIM), x_prompt.dtype)
    yp, ys = x_prompt, x_sample
    per_layer = []
    for layer in range(DEPTH):
        params = (attn_norm_g[layer], w_in[layer], gdn_conv_w[layer], gdn_a_log[layer], gdn_dt_bias[layer],
                  gdn_norm_g[layer], q_norm_g[layer], k_norm_g[layer], cmp_pos_w[layer], cmp_phi[layer],
                  w_o[layer], mlp_norm_g[layer], w_up[layer], w_down[layer])
        yp, cmp_p, slc_p, win_p, conv_p, s_p = hybrid_layer(yp, empty, empty, empty, conv0, s_zero, *params)
        cmp_past = cache_cmp_kv[layer][page_table].reshape((dbsz, past) + kv_tail)
        slc_past = cache_slc_kv[layer][page_table].reshape((dbsz, past) + kv_tail)
        ys, cmp_s, slc_s, win_s, conv_s, s_s = hybrid_layer(
            ys, cmp_past, slc_past, cache_win_kv[layer], cache_gdn_conv[layer], state_gdn[layer], *params)
        per_layer.append((cmp_p, cmp_s, slc_p, slc_s, win_p, win_s, conv_p, conv_s, s_p, s_s))
    st = [jnp.stack(z, axis=0) for z in zip(*per_layer)]
    return (yp, ys, st[0], st[1], st[2], st[3], st[4], st[5], st[6], st[7], st[8], st[9])
```

```python
import numpy as np
import concourse.bass as bass
import concourse.mybir as mybir

F32 = mybir.dt.float32
BF16 = mybir.dt.bfloat16
I32 = mybir.dt.int32
AF = mybir.ActivationFunctionType
ALU = mybir.AluOpType
AX = mybir.AxisListType


class Buf:
    def __init__(self, k, name, t, kind):
        self.k = k
        self.name = name
        self.t = t
        self.kind = kind
        self.last_w = {}
        self.readers = {}
        self.dsem = None
        self.dcnt = 0
        self.war = []
        self.spos = len(k.stack)

    def __getitem__(self, idx):
        return self.t[idx]

    def ap(self):
        return self.t[:] if self.kind != 'dram' else self.t


class K:
    ENG = ['pe', 'dve', 'act', 'pool', 'sp']

    def __init__(self, same_engine_sync=True):
        self.nc = bass.Bass('TRN2', target_bir_lowering=False)
        nc = self.nc
        self.eng = {'pe': nc.tensor, 'dve': nc.vector, 'act': nc.scalar,
                    'pool': nc.gpsimd, 'sp': nc.sync}
        self.stack = []
        self.esem = {}
        self.ecnt = {}
        for e in self.ENG:
            self.esem[e] = self._enter(nc.semaphore('s_' + e))
            self.ecnt[e] = 0
        self.sems = {e: self.esem[e] for e in self.ENG}
        self.known = {e: {} for e in self.ENG}
        self.same_sync = same_engine_sync
        self.n_wait = 0
        self.n_inst = 0
        self._dsem_id = 0
        self.bufs = []
        self.perm = []
        self.free_sems = {'sw': [], 'hw': []}
        self.dcount = {}

    def _enter(self, cm):
        v = cm.__enter__()
        self.stack.append(cm)
        return v

    def close(self):
        while self.stack:
            self.stack.pop().__exit__(None, None, None)

    def sbuf(self, name, shape, dt=F32):
        self._uid = getattr(self, '_uid', 0) + 1
        name = '%s_u%d' % (name, self._uid)
        t = self._enter(self.nc.sbuf_tensor(name, list(shape), dt))
        b = Buf(self, name, t, 'sbuf')
        self.bufs.append(b)
        return b

    def psum(self, name, shape, dt=F32):
        self._uid = getattr(self, '_uid', 0) + 1
        name = '%s_u%d' % (name, self._uid)
        t = self._enter(self.nc.psum_tensor(name, list(shape), dt))
        b = Buf(self, name, t, 'psum')
        self.bufs.append(b)
        return b

    def dram(self, name, shape, dt=F32, kind='Internal'):
        t = self.nc.dram_tensor(name, list(shape), dt, kind=kind).ap()
        b = Buf(self, name, t, 'dram')
        self.bufs.append(b)
        return b

    def scope(self):
        return len(self.stack)

    def barrier(self):
        deps = [(e2, self.ecnt[e2]) for e2 in self.ENG if self.ecnt[e2] > 0]
        for sk, n in self.dcount.items():
            if n > 0:
                deps.append((sk, 16 * n))
        for e in self.ENG:
            self._wait(e, deps)
        for b in self.bufs:
            b.last_w = {}
            b.readers = {}
            b.war = []

    def end_scope(self, mark):
        self.barrier()
        for b in self.bufs:
            if b.kind != 'dram' and b.dsem is not None and b.spos > mark:
                for kd, key in b.dsem.items():
                    self.free_sems[kd].append(key)
                b.dsem = None
        self.bufs = [b for b in self.bufs if b.kind == 'dram' or b.spos <= mark]
        while len(self.stack) > mark:
            self.stack.pop().__exit__(None, None, None)

    def _dsem(self, b, q='sp'):
        kind = 'sw' if q == 'pool' else 'hw'
        if b.dsem is None:
            b.dsem = {}
        if kind not in b.dsem:
            if self.free_sems[kind]:
                b.dsem[kind] = self.free_sems[kind].pop()
            else:
                self._dsem_id += 1
                key = 'd%d' % self._dsem_id
                cm = self.nc.semaphore(key)
                h = cm.__enter__()
                self.perm.append(cm)
                self.sems[key] = h
                self.dcount[key] = 0
                b.dsem[kind] = key
        return b.dsem[kind]

    def _wait(self, e, deps):
        best = {}
        for tok in deps:
            if tok is None:
                continue
            sk, v = tok
            if best.get(sk, 0) < v:
                best[sk] = v
        for sk, v in best.items():
            if sk in self.dcount:
                v = 16 * self.dcount[sk]
            if sk == e:
                if e == 'pe' or not self.same_sync:
                    continue
            if self.known[e].get(sk, 0) >= v:
                continue
            self.eng[e].wait_ge(self.sems[sk], v)
            self.known[e][sk] = v
            self.n_wait += 1

    def _deps(self, reads, writes):
        deps = []
        for b in reads:
            deps.extend(b.last_w.items())
            if b.kind == 'psum':
                deps.extend(b.readers.items())
        for b in writes:
            w = list(b.last_w.items()) + list(b.readers.items())
            deps.extend(w)
            b.war = w
        return deps

    def op(self, e, fn, reads=(), writes=()):
        self._wait(e, self._deps(reads, writes))
        ins = fn(self.eng[e])
        self.ecnt[e] += 1
        tok = (e, self.ecnt[e])
        ins.then_inc(self.esem[e], 1)
        self.n_inst += 1
        for b in writes:
            b.last_w = {tok[0]: tok[1]}
            b.readers = {}
        for b in reads:
            if b not in writes:
                b.readers[e] = tok[1]
        return ins

    def ops(self, e, fns, reads=(), writes=()):
        self._wait(e, self._deps(reads, writes))
        ins = None
        for fn in fns:
            ins = fn(self.eng[e])
            self.n_inst += 1
        self.ecnt[e] += 1
        tok = (e, self.ecnt[e])
        ins.then_inc(self.esem[e], 1)
        for b in writes:
            b.last_w = {tok[0]: tok[1]}
            b.readers = {}
        for b in reads:
            if b not in writes:
                b.readers[e] = tok[1]
        return ins

    def dma(self, q, out_buf, out_ap, in_buf, in_ap, par=None, **kw):
        if par is None:
            par = out_buf.kind == 'dram'
        side = in_buf if (in_buf.kind == 'sbuf' and out_buf.kind == 'dram') else out_buf
        sk = self._dsem(side, q)
        deps = list(in_buf.last_w.items())
        if par and out_buf.last_w and all(a in self.dcount for a in out_buf.last_w):
            deps += list(out_buf.war) + list(out_buf.readers.items())
            keep = True
        else:
            w = list(out_buf.last_w.items()) + list(out_buf.readers.items())
            deps += w
            out_buf.war = w
            keep = False
        self._wait(q, deps)
        ins = self.eng[q].dma_start(out=out_ap, in_=in_ap, **kw)
        self.dcount[sk] = self.dcount.get(sk, 0) + 1
        ins.then_inc(self.sems[sk], 16)
        val = 16 * self.dcount[sk]
        self.n_inst += 1
        if keep:
            out_buf.last_w[sk] = val
        else:
            out_buf.last_w = {sk: val}
            out_buf.readers = {}
        in_buf.readers[sk] = val
        return ins

    def finish(self, bufs, e='sp'):
        deps = []
        for b in bufs:
            deps.extend(b.last_w.items())
        self._wait(e, deps)

def bc_last(ap, n):
    return ap.unsqueeze(len(ap.shape)).broadcast_to(list(ap.shape) + [n])


def bc_mid(ap, n):
    return ap.unsqueeze(1).broadcast_to([ap.shape[0], n, ap.shape[1]])


def gdn_phase(k, C_, featT, small, oT_all, o_stP, o_stS, state_in, convc, convw, alog_b, dtb_b, gnorm, cut=99, nheads=8, ft0=0):
    tri, ntri, ones, nones, ident, negS, negT = (C_[n] for n in ('tri', 'ntri', 'ones', 'nones', 'ident', 'negS', 'negT'))
    ones_bf, eps_t = C_['ones_bf'], C_['eps']
    consts = [tri, ntri, ones, nones, ident, negS, negT]
    sc = k.scope()
    PS = [k.psum('gp%d' % i, [128, 1024], F32) for i in range(4)]
    pi = [0]

    def nps():
        p = PS[pi[0] % 4]
        pi[0] += 1
        return p

    def act(out, in_, func, r, w, **kw):
        k.op('act', lambda e: e.activation(out=out, in_=in_, func=func, **kw), reads=r, writes=w)

    def tt(eng, out, in0, in1, op, r, w):
        k.op(eng, lambda e: e.tensor_tensor(out=out, in0=in0, in1=in1, op=op), reads=r, writes=w)

    convw_s = k.sbuf('convw_s', [128, NH, 3, 4], F32)
    k.dma('sp', convw_s, convw_s[:], convw, convw.t)
    convc_s = k.sbuf('convc_s', [128, NH, 3, NS, 3], F32)
    k.dma('sp', convc_s, convc_s[:], convc, convc.t)
    alog_s = k.sbuf('alog_s', [128, NH], F32)
    k.dma('sp', alog_s, alog_s[:], alog_b, alog_b.t)
    dtb_s = k.sbuf('dtb_s', [128, NH], F32)
    k.dma('sp', dtb_s, dtb_s[:], dtb_b, dtb_b.t)
    gn_s = k.sbuf('gn_s', [128, 1], F32)
    k.dma('sp', gn_s, gn_s[:], gnorm, gnorm.t)
    nA = k.sbuf('nA', [128, NH], F32)
    act(nA[:], alog_s[:], AF.Exp, [alog_s], [nA])
    k.op('dve', lambda e: e.tensor_scalar(out=nA[:], in0=nA[:], scalar1=-1.0, scalar2=None, op0=ALU.mult), reads=[nA], writes=[nA])

    if cut <= 1:
        k.end_scope(sc)
        return
    def gate_prep(tag, Cp, NC, src_ap):
        raw = k.sbuf('raw' + tag, [Cp, NC, 16], F32)
        k.dma('sp', raw, raw[:], small, src_ap)
        G = {}
        for nm in ('beta', 'nbeta', 'g', 'gcum', 'bg', 'kd'):
            G[nm] = k.sbuf(nm + tag, [Cp, NC, NH], F32)
        G['egl'] = k.sbuf('egl' + tag, [128, NC, NH], F32)
        act(G['beta'][:], raw[:, :, 0:8], AF.Sigmoid, [raw], [G['beta']])
        k.op('dve', lambda e: e.tensor_scalar(out=G['nbeta'][:], in0=G['beta'][:], scalar1=-1.0, scalar2=None, op0=ALU.mult),
             reads=[G['beta']], writes=[G['nbeta']])
        import os
        gc_ = int(os.environ.get('GCUT', '9'))
        if gc_ <= 1:
            return G
        g = G['g']
        tt('dve', g[:], raw[:, :, 8:16], bc_mid(dtb_s[0:Cp, :], NC), ALU.add, [raw, dtb_s], [g])
        act(g[:], g[:], AF.Exp, [g], [g])
        act(g[:], g[:], AF.Ln, [g], [g], bias=1.0)
        tt('dve', g[:], g[:], bc_mid(nA[0:Cp, :], NC), ALU.mult, [g, nA], [g])
        if gc_ <= 2:
            return G
        gf = g[:].rearrange('p c h -> p (c h)')
        p1 = nps()
        k.op('pe', lambda e: e.matmul(p1[0:Cp, 0:NC * NH], lhsT=tri[0:Cp, 0:Cp], rhs=gf, start=True, stop=True), reads=[tri, g], writes=[p1])
        gc = G['gcum']
        k.op('dve', lambda e: e.tensor_copy(out=gc[:].rearrange('p c h -> p (c h)'), in_=p1[0:Cp, 0:NC * NH]), reads=[p1], writes=[gc])
        if gc_ <= 3:
            return G
        p2 = nps()
        k.op('pe', lambda e: e.matmul(p2[0:128, 0:NC * NH], lhsT=ones[0:Cp, 0:128], rhs=gf, start=True, stop=True), reads=[ones, g], writes=[p2])
        act(G['egl'][:].rearrange('p c h -> p (c h)'), p2[0:128, 0:NC * NH], AF.Exp, [p2], [G['egl']])
        if gc_ <= 4 or (os.environ.get('GSK') == tag):
            return G
        kd = G['kd']
        p3 = nps()
        k.ops('pe', [lambda e: e.matmul(p3[0:Cp, 0:NC * NH], lhsT=ones[0:Cp, 0:Cp], rhs=gf, start=True, stop=False),
                     lambda e: e.matmul(p3[0:Cp, 0:NC * NH], lhsT=ntri[0:Cp, 0:Cp], rhs=gf, start=False, stop=True)],
              reads=[ones, ntri, g], writes=[p3])
        act(kd[:].rearrange('p c h -> p (c h)'), p3[0:Cp, 0:NC * NH], AF.Exp, [p3], [kd])
        if gc_ <= 5:
            return G
        bg = G['bg']
        act(bg[:], gc[:], AF.Exp, [gc], [bg])
        tt('dve', bg[:], bg[:], G['beta'][:], ALU.mult, [bg, G['beta']], [bg])
        return G

    GP = gate_prep('P', 64, 32, small.t[0:SEQ, 0:16].rearrange('(c p) n -> p c n', p=64))
    GS = gate_prep('S', 4, NS, small.t[SEQ:TOK, 0:16].rearrange('(s p) n -> p s n', p=4))

    if cut <= 2:
        k.end_scope(sc)
        return
    XW = 3 + SEQ + 7 * NS
    cx = k.sbuf('cx', [128, 3, XW], F32)
    cy = k.sbuf('cy', [128, 3, TOK], F32)
    zs = k.sbuf('zs', [128, TOK], F32)
    oTh = k.sbuf('oTh', [128, TOK], F32)
    sqb = k.sbuf('sqb', [128, TOK], BF16)
    rst = k.sbuf('rst', [128, TOK], F32)
    ob = k.sbuf('ob', [128, TOK], BF16)
    Sb = [k.sbuf('S%d' % i, [128, 128], F32) for i in range(2)]
    GO8 = k.sbuf('GO8', [64, 8, 64], F32); GT8 = k.sbuf('GT8', [64, 8, 64], F32)
    decS = k.sbuf('decS', [64, 8, 64], F32); decT = k.sbuf('decT', [64, 8, 64], F32)
    EG = k.sbuf('EG', [128, 8, 64], F32)
    Pm = [k.sbuf('Pm%d' % i, [64, 8, 64], F32) for i in range(2)]
    PTm = [k.sbuf('PTm%d' % i, [64, 8, 64], F32) for i in range(2)]
    XT = k.sbuf('XT', [64, 8, 64], F32)
    aqk = k.sbuf('aqk', [64, 8, 64], F32)
    kbg = k.sbuf('kbg', [64, 8, 128], F32); kdc = k.sbuf('kdc', [64, 8, 128], F32); vb = k.sbuf('vb', [64, 8, 128], F32)
    u8 = k.sbuf('u8', [64, 8, 128], F32)
    wT8 = k.sbuf('wT8', [128, 8, 64], F32); qg8 = k.sbuf('qg8', [128, 8, 64], F32)
    vn = [k.sbuf('vn%d' % i, [64, 128], F32) for i in range(2)]
    k.op('dve', lambda e: e.memset(cx[:, :, 0:3], 0.0), writes=[cx])

    def rstd_bc(src_sq_in, n_scale, bias_eps=True):
        pass

    def batch(hd, C, col0, Gd, ci0, chained, s_state):
        W8 = 8 * C
        import os
        bcut = int(os.environ.get('BCUT', '99'))
        Tm, nTm, On, nOn, Id = tri[0:C, 0:C], ntri[0:C, 0:C], ones[0:C, 0:C], nones[0:C, 0:C], ident[0:C, 0:C]
        g8 = Gd['g'][0:C, ci0:ci0 + 8, hd]
        qT8 = cy[:, 0, col0:col0 + W8].rearrange('p (c j) -> p c j', j=C)
        kT8 = cy[:, 1, col0:col0 + W8].rearrange('p (c j) -> p c j', j=C)
        vT8 = cy[:, 2, col0:col0 + W8].rearrange('p (c j) -> p c j', j=C)
        v3 = lambda b_, P_=C, F_=C: b_[0:P_, :, 0:F_]
        pv = lambda p_, P_, F_: p_[0:P_, 0:8 * F_].rearrange('p (c j) -> p c j', j=F_)
        k.op('dve', lambda e: e.tensor_copy(out=v3(GO8), in_=bc_last(g8, C)), reads=[Gd['g']], writes=[GO8])
        tt('dve', v3(GT8), bc_mid(Tm, 8), bc_last(g8, C), ALU.mult, [tri, Gd['g']], [GT8])
        if bcut <= 1:
            return
        pD = nps()
        fns = []
        for c in range(8):
            o_ = pv(pD, C, C)[:, c, :]
            fns.append(lambda e, o_=o_, c=c: e.matmul(o_, lhsT=Tm, rhs=GO8[0:C, c, 0:C], start=True, stop=False))
            fns.append(lambda e, o_=o_, c=c: e.matmul(o_, lhsT=nOn, rhs=GT8[0:C, c, 0:C], start=False, stop=False))
            fns.append(lambda e, o_=o_, c=c: e.matmul(o_, lhsT=Id, rhs=negS[0:C, 0:C], start=False, stop=True))
        k.ops('pe', fns, reads=consts + [GO8, GT8], writes=[pD])
        act(v3(decS), pv(pD, C, C), AF.Exp, [pD], [decS])
        pDT = nps()
        fns = []
        for c in range(8):
            o_ = pv(pDT, C, C)[:, c, :]
            fns.append(lambda e, o_=o_, c=c: e.matmul(o_, lhsT=On, rhs=GT8[0:C, c, 0:C], start=True, stop=False))
            fns.append(lambda e, o_=o_, c=c: e.matmul(o_, lhsT=nTm, rhs=GO8[0:C, c, 0:C], start=False, stop=False))
            fns.append(lambda e, o_=o_, c=c: e.matmul(o_, lhsT=Id, rhs=negT[0:C, 0:C], start=False, stop=True))
        k.ops('pe', fns, reads=consts + [GO8, GT8], writes=[pDT])
        act(v3(decT), pv(pDT, C, C), AF.Exp, [pDT], [decT])
        pE = nps()
        k.ops('pe', [(lambda e, c=c: e.matmul(pv(pE, 128, C)[:, c, :], lhsT=ones[0:C, 0:128], rhs=GT8[0:C, c, 0:C], start=True, stop=True)) for c in range(8)],
              reads=[ones, GT8], writes=[pE])
        act(EG[:, :, 0:C], pv(pE, 128, C), AF.Exp, [pE], [EG])
        tt('dve', qg8[:, :, 0:C], qT8, EG[:, :, 0:C], ALU.mult, [cy, EG], [qg8])
        if bcut <= 2:
            return
        pA = nps()
        k.ops('pe', [(lambda e, c=c: e.matmul(pv(pA, C, C)[:, c, :], lhsT=kT8[:, c, :], rhs=kT8[:, c, :], start=True, stop=True)) for c in range(8)],
              reads=[cy], writes=[pA])
        P0, PT0 = Pm[0], PTm[0]
        tt('dve', v3(P0), pv(pA, C, C), bc_last(Gd['nbeta'][0:C, ci0:ci0 + 8, hd], C), ALU.mult, [pA, Gd['nbeta']], [P0])
        tt('dve', v3(P0), v3(P0), v3(decS), ALU.mult, [P0, decS], [P0])
        pQ = nps()
        k.ops('pe', [(lambda e, c=c: e.matmul(pv(pQ, C, C)[:, c, :], lhsT=kT8[:, c, :], rhs=qT8[:, c, :], start=True, stop=True)) for c in range(8)],
              reads=[cy], writes=[pQ])
        tt('dve', v3(aqk), pv(pQ, C, C), v3(decT), ALU.mult, [pQ, decT], [aqk])
        if bcut <= 3:
            return
        pT = nps()
        k.ops('pe', [(lambda e, c=c: e.transpose(pv(pT, C, C)[:, c, :], P0[0:C, c, 0:C], Id)) for c in range(8)],
              reads=[P0, ident], writes=[pT])
        k.op('act', lambda e: e.activation(out=v3(PT0), in_=pv(pT, C, C), func=AF.Copy), reads=[pT], writes=[PT0])
        tt('dve', v3(XT), pv(pT, C, C), bc_mid(Id, 8), ALU.add, [pT, ident], [XT])
        if bcut <= 4:
            return
        nr = {64: 5, 4: 1}[C]
        cur = 0
        for r in range(nr):
            Pc, PTc, Pn, PTn = Pm[cur], PTm[cur], Pm[1 - cur], PTm[1 - cur]
            pP = nps()
            k.ops('pe', [(lambda e, c=c: e.matmul(pv(pP, C, C)[:, c, :], lhsT=PTc[0:C, c, 0:C], rhs=Pc[0:C, c, 0:C], start=True, stop=True)) for c in range(8)],
                  reads=[Pc, PTc], writes=[pP])
            if r < nr - 1:
                pPT = nps()
                k.ops('pe', [(lambda e, c=c: e.matmul(pv(pPT, C, C)[:, c, :], lhsT=Pc[0:C, c, 0:C], rhs=PTc[0:C, c, 0:C], start=True, stop=True)) for c in range(8)],
                      reads=[Pc, PTc], writes=[pPT])
            k.op('act', lambda e: e.activation(out=v3(Pn), in_=pv(pP, C, C), func=AF.Copy), reads=[pP], writes=[Pn])
            if r < nr - 1:
                k.op('dve', lambda e: e.tensor_copy(out=v3(PTn), in_=pv(pPT, C, C)), reads=[pPT], writes=[PTn])
            pX = nps()
            k.ops('pe', [(lambda e, c=c: e.matmul(pv(pX, C, C)[:, c, :], lhsT=Pn[0:C, c, 0:C], rhs=XT[0:C, c, 0:C], start=True, stop=True)) for c in range(8)],
                  reads=[Pn, XT], writes=[pX])
            tt('dve', v3(XT), v3(XT), pv(pX, C, C), ALU.add, [XT, pX], [XT])
            cur = 1 - cur
        if bcut <= 5:
            return
        pK = nps()
        k.ops('pe', [(lambda e, c=c: e.transpose(pv(pK, C, 128)[:, c, :], kT8[:, c, :], ident[:, :])) for c in range(8)],
              reads=[cy, ident], writes=[pK])
        tt('dve', kbg[0:C, :, :], pv(pK, C, 128), bc_last(Gd['bg'][0:C, ci0:ci0 + 8, hd], 128), ALU.mult, [pK, Gd['bg']], [kbg])
        tt('dve', kdc[0:C, :, :], pv(pK, C, 128), bc_last(Gd['kd'][0:C, ci0:ci0 + 8, hd], 128), ALU.mult, [pK, Gd['kd']], [kdc])
        pV = nps()
        k.ops('pe', [(lambda e, c=c: e.transpose(pv(pV, C, 128)[:, c, :], vT8[:, c, :], ident[:, :])) for c in range(8)],
              reads=[cy, ident], writes=[pV])
        tt('dve', vb[0:C, :, :], pv(pV, C, 128), bc_last(Gd['beta'][0:C, ci0:ci0 + 8, hd], 128), ALU.mult, [pV, Gd['beta']], [vb])
        if bcut <= 6:
            return
        pU = nps()
        k.ops('pe', [(lambda e, c=c: e.matmul(pv(pU, C, 128)[:, c, :], lhsT=XT[0:C, c, 0:C], rhs=vb[0:C, c, :], start=True, stop=True)) for c in range(8)],
              reads=[XT, vb], writes=[pU])
        k.op('act', lambda e: e.activation(out=u8[0:C, :, :], in_=pv(pU, C, 128), func=AF.Copy), reads=[pU], writes=[u8])
        pW = nps()
        k.ops('pe', [(lambda e, c=c: e.matmul(pv(pW, 128, C)[:, c, :], lhsT=kbg[0:C, c, :], rhs=XT[0:C, c, 0:C], start=True, stop=True)) for c in range(8)],
              reads=[kbg, XT], writes=[pW])
        k.op('act', lambda e: e.activation(out=wT8[:, :, 0:C], in_=pv(pW, 128, C), func=AF.Copy), reads=[pW], writes=[wT8])
        if bcut <= 7:
            return
        for c in range(8):
            if not chained:
                S = Sb[c % 2]
                k.dma('sp', S, S[:], state_in, state_in.t[c, hd])
                Sn = S
            else:
                S = s_state[0]
                Sn = Sb[1] if S is Sb[0] else Sb[0]
            v_ = vn[c % 2]
            pws = nps()
            k.op('pe', lambda e: e.matmul(pws[0:C, 0:128], lhsT=wT8[:, c, 0:C], rhs=S[:], start=True, stop=True), reads=[wT8, S], writes=[pws])
            tt('dve', v_[0:C, :], u8[0:C, c, :], pws[0:C, 0:128], ALU.subtract, [u8, pws], [v_])
            po = nps()
            k.ops('pe', [lambda e: e.matmul(po[:, 0:C], lhsT=S[:], rhs=qg8[:, c, 0:C], start=True, stop=False),
                         lambda e: e.matmul(po[:, 0:C], lhsT=v_[0:C, :], rhs=aqk[0:C, c, 0:C], start=False, stop=True)],
                  reads=[S, qg8, v_, aqk], writes=[po])
            k.op('act', lambda e: e.activation(out=oTh[:, col0 + c * C:col0 + (c + 1) * C], in_=po[:, 0:C], func=AF.Copy), reads=[po], writes=[oTh])
            pdS = nps()
            k.op('pe', lambda e: e.matmul(pdS[:, 0:128], lhsT=kdc[0:C, c, :], rhs=v_[0:C, :], start=True, stop=True), reads=[kdc, v_], writes=[pdS])
            k.op('dve', lambda e: e.scalar_tensor_tensor(out=Sn[:], in0=S[:], scalar=Gd['egl'][:, ci0 + c, hd:hd + 1], in1=pdS[:, 0:128],
                                                          op0=ALU.mult, op1=ALU.add), reads=[S, Gd['egl'], pdS], writes=[Sn])
            if chained:
                s_state[0] = Sn
            else:
                k.dma('sp', o_stS, o_stS.t[c, hd], Sn, Sn[:])

    for hd in range(nheads):
        for q3 in range(3):
            k.dma('sp', cx, cx[:, q3, 3:3 + SEQ], featT, featT.t[hd, q3, :, 0:SEQ], par=True)
            k.dma('sp', cx, cx[:, q3, 3 + SEQ:XW].rearrange('p (s j) -> p s j', j=7)[:, :, 3:7], featT,
                  featT.t[hd, q3, :, SEQ:TOK].rearrange('p (s j) -> p s j', j=4), par=True)
        k.op('pool', lambda e: e.tensor_copy(out=cx[:, :, 3 + SEQ:XW].rearrange('p q (s j) -> p q s j', j=7)[:, :, :, 0:3], in_=convc_s[:, hd]),
             reads=[convc_s], writes=[cx])
        k.dma('sp', zs, zs[:], featT, featT.t[hd, 3])
        for q3 in range(3):
            for (ov, iv) in ((cy[:, q3, 0:SEQ], lambda j: cx[:, q3, j:j + SEQ]),
                             (cy[:, q3, SEQ:TOK].rearrange('p (s j) -> p s j', j=4),
                              lambda j: cx[:, q3, 3 + SEQ:XW].rearrange('p (s j) -> p s j', j=7)[:, :, j:j + 4])):
                k.op('dve', lambda e: e.tensor_scalar(out=ov, in0=iv(0), scalar1=convw_s[:, hd, q3, 0:1], scalar2=None, op0=ALU.mult),
                     reads=[cx, convw_s], writes=[cy])
                for j in range(1, 4):
                    k.op('dve', lambda e: e.scalar_tensor_tensor(out=ov, in0=iv(j), scalar=convw_s[:, hd, q3, j:j + 1], in1=ov,
                                                                  op0=ALU.mult, op1=ALU.add), reads=[cx, convw_s, cy], writes=[cy])
            act(cy[:, q3, :], cy[:, q3, :], AF.Silu, [cy], [cy])
        if cut <= 3:
            continue
        for q3 in range(2):
            act(sqb[:], cy[:, q3, :], AF.Square, [cy], [sqb])
            for (c0, n) in CH:
                p = nps()
                k.op('pe', lambda e: e.matmul(p[:, 0:n], lhsT=ones_bf[:], rhs=sqb[:, c0:c0 + n], start=True, stop=True), reads=[ones_bf, sqb], writes=[p])
                act(rst[:, c0:c0 + n], p[:, 0:n], AF.Ln, [p, eps_t], [rst], bias=eps_t[:, :])
            act(rst[:], rst[:], AF.Exp, [rst], [rst], scale=-0.5)
            k.op('dve', lambda e: e.scalar_tensor_tensor(out=cy[:, q3, :], in0=cy[:, q3, :], scalar=(HD ** -0.5 if q3 == 0 else 1.0), in1=rst[:],
                                                          op0=ALU.mult, op1=ALU.mult), reads=[cy, rst], writes=[cy])
        if cut <= 4:
            continue
        k.op('dve', lambda e: e.memset(Sb[0][:], 0.0), writes=[Sb[0]])
        st = [Sb[0]]
        for bt in range(4 if cut > 5 else 1):
            batch(hd, 64, bt * 512, GP, bt * 8, True, st)
        if cut <= 6:
            continue
        k.dma('sp', o_stP, o_stP.t[hd], st[0], st[0][:])
        batch(hd, 4, SEQ, GS, 0, False, None)
        act(sqb[:], oTh[:], AF.Square, [oTh], [sqb])
        for (c0, n) in CH:
            p = nps()
            k.op('pe', lambda e: e.matmul(p[:, 0:n], lhsT=ones_bf[:], rhs=sqb[:, c0:c0 + n], start=True, stop=True), reads=[ones_bf, sqb], writes=[p])
            act(rst[:, c0:c0 + n], p[:, 0:n], AF.Ln, [p, eps_t], [rst], bias=eps_t[:, :], scale=1.0 / HD)
        act(rst[:], rst[:], AF.Exp, [rst], [rst], scale=-0.5)
        act(zs[:], zs[:], AF.Silu, [zs], [zs])
        k.op('dve', lambda e: e.scalar_tensor_tensor(out=zs[:], in0=zs[:], scalar=gn_s[:, 0:1], in1=rst[:], op0=ALU.mult, op1=ALU.mult),
             reads=[zs, gn_s, rst], writes=[zs])
        tt('dve', ob[:], oTh[:], zs[:], ALU.mult, [oTh, zs], [ob])
        k.dma('sp', oT_all, oT_all.t[ft0 + hd], ob, ob[:])
    k.end_scope(sc)

NEG_TINY = 1e-30
SCALE = 128 ** -0.5


def nsa_phase(k, C_, hh, T, do_prompt=True, do_sample=True, ngroups=2, nseq=8):
    ones_bf, eps_t, ident = C_['ones_bf'], C_['eps'], C_['ident']
    qn, small, o_cmp, o_slc, o_win, oT_all = (T[n] for n in ('qn', 'small', 'o_cmp', 'o_slc', 'o_win', 'oT_all'))
    sc = k.scope()
    ACC = [k.psum('acc%d' % i, [128, 512], F32) for i in range(4)]
    PSF = [k.psum('psf%d' % i, [128, 512], F32) for i in range(4)]
    pi = [0]

    def nps():
        p = PSF[pi[0] % 4]
        pi[0] += 1
        return p

    def act(out, in_, func, r, w, **kw):
        k.op('act', lambda e: e.activation(out=out, in_=in_, func=func, **kw), reads=r, writes=w)

    def tt(eng, out, in0, in1, op, r, w):
        k.op(eng, lambda e: e.tensor_tensor(out=out, in0=in0, in1=in1, op=op), reads=r, writes=w)

    def ts(eng, out, in0, s1, op0, r, w, s2=None, op1=None):
        if op1 is None:
            k.op(eng, lambda e: e.tensor_scalar(out=out, in0=in0, scalar1=s1, scalar2=None, op0=op0), reads=r, writes=w)
        else:
            k.op(eng, lambda e: e.tensor_scalar(out=out, in0=in0, scalar1=s1, scalar2=s2, op0=op0, op1=op1), reads=r, writes=w)

    def cload(name, shape, dt=BF16, src=None):
        b = k.sbuf('n_' + name, shape, dt)
        s_ = T[name] if src is None else src
        k.dma('pool' if dt == BF16 else 'sp', b, b[:], T[name], s_ if src is not None else T[name].t)
        return b

    maskC = cload('maskC', [128, SEQ])
    causD = cload('causD', [128, 4, 512])
    winD = cload('winD', [128, 8, 512])
    ExAll = cload('ExAll', [32, 16, 128])
    ovP = cload('ovP', [128, 32])
    keepP = cload('keepP', [128, 16, 32], F32)
    biasP = cload('biasP', [128, 16, 32], F32)
    gk0c = cload('gk0c', [128, 1], F32)
    wpool = cload('wpool', [128, 2, 2, 16], BF16, src=T['wpool'].t[hh])
    phi = cload('phi', [128, 2, 2, 128], BF16, src=T['phi'].t[hh])
    ovS = cload('ovS', [128, 4, 129])
    keepS = cload('keepS', [32, 129], F32)
    biasS = cload('biasS', [32, 129], F32)
    ExS = cload('ExS', [128, 64, 128])
    maskWS = cload('maskWS', [128, 4, 32])
    maskN = cload('maskNs', [32, 8, 32])
    iota_p = cload('iota_p', [128, 1], F32)
    pts = k.sbuf('pts', [128, NS * 64], I32)
    k.dma('sp', pts, pts[:], T['page_tab'], T['page_tab'].t.partition_broadcast(128))
    idx = k.sbuf('idx', [128, NS * 64], I32)
    ts('dve', idx[:], pts[:], 128.0, ALU.mult, [pts, iota_p], [idx], s2=iota_p[:, 0:1], op1=ALU.add)

    QT = k.sbuf('QT', [128, 4, TOK], BF16)
    KsT = k.sbuf('KsT', [128, TOK], BF16)
    KwT = k.sbuf('KwT', [128, TOK], BF16)
    Vs = k.sbuf('Vs', [128, 17, 129], BF16)
    Vw = k.sbuf('Vw', [128, 17, 129], BF16)
    Xk = k.sbuf('Xk', [128, 16, 128], BF16)
    Xv = k.sbuf('Xv', [128, 16, 128], BF16)
    gs = k.sbuf('gs', [128, 17, 12], F32)
    oacc = k.sbuf('oacc', [128, 17, 4, 128], F32)
    imp = k.sbuf('imp', [128, 16, 32], F32)
    sel = k.sbuf('sel', [128, 16, 32], F32)
    selT = k.sbuf('selT', [32, SEQ], BF16)
    stg = [k.sbuf('nstg%d' % i, [128, 768], F32) for i in range(2)]
    pooled = k.sbuf('pooled', [128, 2, 128], BF16)
    ptmp = k.sbuf('ptmp', [128, 2, 128], F32)
    sqc = k.sbuf('sqc', [128, 512], BF16)
    rsc = k.sbuf('rsc', [128, 512], F32)
    kcn = k.sbuf('kcn', [128, 512], BF16)
    vcaug = k.sbuf('vcaug', [128, 161], BF16)
    Eb = [k.sbuf('Eb%d' % i, [128, 512], BF16) for i in range(3)]
    ei = [0]
    rz = [k.sbuf('rz%d' % i, [128, 2], F32) for i in range(4)]
    ri = [0]
    m8 = k.sbuf('m8', [128, 16], F32)
    sctmp = k.sbuf('sctmp', [128, 32], F32)
    oTs = k.sbuf('oTs', [128, 4, TOK], BF16)
    k.op('dve', lambda e: e.memset(Vs[:, :, 128:129], 1.0), writes=[Vs])
    k.op('dve', lambda e: e.memset(Vw[:, :, 128:129], 1.0), writes=[Vw])
    k.op('dve', lambda e: e.memset(vcaug[:], 1.0), writes=[vcaug])
    k.op('dve', lambda e: e.tensor_copy(out=vcaug[:, 128:160], in_=ovP[:]), reads=[ovP], writes=[vcaug])

    def nextE():
        b = Eb[ei[0] % 3]
        ei[0] += 1
        return b

    def nextrz():
        b = rz[ri[0] % 4]
        ri[0] += 1
        return b

    def finish_acc(p, ncol, nt, o_dst, gate_ap, first, extra=None):
        r_ = nextrz()
        ts('dve', r_[0:nt, 0:1], p[0:nt, ncol:ncol + 1], NEG_TINY, ALU.max, [p], [r_])
        k.op('dve', lambda e: e.reciprocal(out=r_[0:nt, 0:1], in_=r_[0:nt, 0:1]), reads=[r_], writes=[r_])
        tt('dve', r_[0:nt, 1:2], r_[0:nt, 0:1], gate_ap, ALU.mult, [r_, gs], [r_])
        if first:
            ts('dve', o_dst, p[0:nt, 0:128], r_[0:nt, 1:2], ALU.mult, [p, r_], [oacc])
        else:
            k.op('dve', lambda e: e.scalar_tensor_tensor(out=o_dst, in0=p[0:nt, 0:128], scalar=r_[0:nt, 1:2], in1=o_dst, op0=ALU.mult, op1=ALU.add),
                 reads=[p, r_, oacc], writes=[oacc])
        return r_

    def rms_cols(pin, ncols, gcol, out_bf):
        act(sqc[:, 0:ncols], pin[:, 0:ncols], AF.Square, [pin], [sqc])
        p2 = nps()
        k.op('pe', lambda e: e.matmul(p2[:, 0:ncols], lhsT=ones_bf[:], rhs=sqc[:, 0:ncols], start=True, stop=True), reads=[ones_bf, sqc], writes=[p2])
        act(rsc[:, 0:ncols], p2[:, 0:ncols], AF.Ln, [p2, eps_t], [rsc], bias=eps_t[:, :], scale=1.0 / HD)
        act(rsc[:, 0:ncols], rsc[:, 0:ncols], AF.Exp, [rsc], [rsc], scale=-0.5)
        k.op('dve', lambda e: e.scalar_tensor_tensor(out=out_bf, in0=pin[:, 0:ncols], scalar=gcol, in1=rsc[:, 0:ncols], op0=ALU.mult, op1=ALU.mult),
             reads=[pin, gk0c, rsc], writes=[kcn])

    for gl in range(ngroups):
        for ti, (t0, nt) in enumerate(TT):
            st = stg[ti % 2]
            k.dma('sp', st, st[0:nt, 0:512], qn, qn.t[t0:t0 + nt, gl * 512:(gl + 1) * 512])
            k.dma('sp', st, st[0:nt, 512:640], o_slc, o_slc.t[t0:t0 + nt, gl * 128:(gl + 1) * 128], par=True)
            k.dma('sp', st, st[0:nt, 640:768], o_win, o_win.t[t0:t0 + nt, gl * 128:(gl + 1) * 128], par=True)
            pa = nps()
            k.ops('pe', [(lambda e, j=j: e.transpose(pa[:, j * 128:j * 128 + nt], st[0:nt, j * 128:(j + 1) * 128], ident[0:nt, 0:nt])) for j in range(4)],
                  reads=[st, ident], writes=[pa])
            act(QT[:, :, t0:t0 + nt], pa[:, :].rearrange('p (r t) -> p r t', t=128)[:, :, 0:nt], AF.Copy, [pa], [QT])
            pb = nps()
            k.ops('pe', [(lambda e, j=j: e.transpose(pb[:, j * 128:j * 128 + nt], st[0:nt, 512 + j * 128:640 + j * 128], ident[0:nt, 0:nt])) for j in range(2)],
                  reads=[st, ident], writes=[pb])
            k.op('dve', lambda e: e.tensor_copy(out=KsT[:, t0:t0 + nt], in_=pb[:, 0:nt]), reads=[pb], writes=[KsT])
            k.op('dve', lambda e: e.tensor_copy(out=KwT[:, t0:t0 + nt], in_=pb[:, 128:128 + nt]), reads=[pb], writes=[KwT])
            k.dma('pool', Vs, Vs[0:nt, ti, 0:128], o_slc, o_slc.t[t0:t0 + nt, 256 + gl * 128:256 + (gl + 1) * 128], par=True)
            k.dma('pool', Vw, Vw[0:nt, ti, 0:128], o_win, o_win.t[t0:t0 + nt, 256 + gl * 128:256 + (gl + 1) * 128], par=True)
        k.dma('pool', Xk, Xk[:], o_cmp, o_cmp.t[0:SEQ, gl * 128:(gl + 1) * 128].rearrange('(t p) d -> p t d', p=128))
        k.dma('pool', Xv, Xv[:], o_cmp, o_cmp.t[0:SEQ, 256 + gl * 128:256 + (gl + 1) * 128].rearrange('(t p) d -> p t d', p=128))
        k.op('dve', lambda e: e.memset(gs[:], 0.0), writes=[gs])
        for ti, (t0, nt) in enumerate(TT):
            k.dma('sp', gs, gs[0:nt, ti, :], small, small.t[t0:t0 + nt, 16 + 12 * gl:28 + 12 * gl], par=True)
        act(gs[:], gs[:], AF.Sigmoid, [gs], [gs])

        if do_prompt:
            for kv, X in ((0, Xk), (1, Xv)):
                pp_ = nps()
                k.ops('pe', [(lambda e, t_=t_: e.matmul(pp_[:, t_ * 16:(t_ + 1) * 16], lhsT=X[:, t_, :], rhs=wpool[:, gl, kv, :], start=True, stop=True)) for t_ in range(16)],
                      reads=[X, wpool], writes=[pp_])
                pv_ = pp_[:, 0:256].rearrange('p (t c j) -> p t c j', c=2, j=8)
                act(ptmp[:, :, :].rearrange('p c (t j) -> p t c j', j=8), pv_, AF.Copy, [pp_], [ptmp])
                tt('dve', pooled[:, kv, 0:127], ptmp[:, 0, 0:127], ptmp[:, 1, 1:128], ALU.add, [ptmp], [pooled])
            pkc = nps()
            k.op('pe', lambda e: e.matmul(pkc[:, 0:127], lhsT=phi[:, gl, 0, :], rhs=pooled[:, 0, 0:127], start=True, stop=True), reads=[phi, pooled], writes=[pkc])
            rms_cols(pkc, 127, gk0c[:, 0:1], kcn[:, 0:127])
            pvc = nps()
            k.op('pe', lambda e: e.matmul(pvc[0:127, 0:128], lhsT=pooled[:, 1, 0:127], rhs=phi[:, gl, 1, :], start=True, stop=True), reads=[phi, pooled], writes=[pvc])
            act(vcaug[0:127, 0:128], pvc[0:127, 0:128], AF.Copy, [pvc], [vcaug])
            for r in range(4):
                for qc in range(4):
                    ps_ = nps()
                    k.op('pe', lambda e: e.matmul(ps_[0:127, :], lhsT=kcn[:, 0:127], rhs=QT[:, r, qc * 512:(qc + 1) * 512], start=True, stop=True),
                         reads=[kcn, QT], writes=[ps_])
                    E = nextE()
                    act(E[0:127, :], ps_[0:127, :], AF.Exp, [ps_], [E], scale=SCALE)
                    tt('pool', E[0:127, :], E[0:127, :], maskC[0:127, qc * 512:(qc + 1) * 512], ALU.mult, [E, maskC], [E])
                    for sub in range(4):
                        ti = qc * 4 + sub
                        po = nps()
                        k.op('pe', lambda e: e.matmul(po[:, 0:161], lhsT=E[0:127, sub * 128:(sub + 1) * 128], rhs=vcaug[0:127, :], start=True, stop=True),
                             reads=[E, vcaug], writes=[po])
                        r_ = finish_acc(po, 160, 128, oacc[:, ti, r, :], gs[:, ti, 3 * r:3 * r + 1], True)
                        if r == 0:
                            ts('dve', imp[:, ti, :], po[:, 128:160], r_[:, 0:1], ALU.mult, [po, r_], [imp])
                        else:
                            k.op('dve', lambda e: e.scalar_tensor_tensor(out=imp[:, ti, :], in0=po[:, 128:160], scalar=r_[:, 0:1], in1=imp[:, ti, :],
                                                                          op0=ALU.mult, op1=ALU.add), reads=[po, r_, imp], writes=[imp])
            tt('dve', imp[:], imp[:], keepP[:], ALU.mult, [imp, keepP], [imp])
            tt('dve', imp[:], imp[:], biasP[:], ALU.add, [imp, biasP], [imp])
            for ti in range(16):
                k.op('dve', lambda e: e.max(out=m8[:, 0:8], in_=imp[:, ti, :]), reads=[imp], writes=[m8])
                k.op('dve', lambda e: e.match_replace(out=sctmp[:], in_to_replace=m8[:, 0:8], in_values=imp[:, ti, :], imm_value=-3.0e38), reads=[m8, imp], writes=[sctmp])
                k.op('dve', lambda e: e.max(out=m8[:, 8:16], in_=sctmp[:]), reads=[sctmp], writes=[m8])
                ts('dve', sel[:, ti, :], imp[:, ti, :], m8[:, 15:16], ALU.is_ge, [imp, m8], [sel])
            for q4 in range(4):
                pt_ = nps()
                k.ops('pe', [(lambda e, j=j: e.transpose(pt_[0:32, j * 128:(j + 1) * 128], sel[:, q4 * 4 + j, :], ident[:, :])) for j in range(4)],
                      reads=[sel, ident], writes=[pt_])
                act(selT[:, q4 * 512:(q4 + 1) * 512], pt_[0:32, :], AF.Copy, [pt_], [selT])
            for r in range(4):
                for qc in range(4):
                    for br in range(2):
                        KT_, V_ = (KsT, Vs) if br == 0 else (KwT, Vw)
                        kts = list(range(0, 4 * qc + 4)) if br == 0 else list(range(max(0, 4 * qc - 4), 4 * qc + 4))
                        for ki, kt in enumerate(kts):
                            ps_ = nps()
                            k.op('pe', lambda e: e.matmul(ps_[:, :], lhsT=KT_[:, kt * 128:(kt + 1) * 128], rhs=QT[:, r, qc * 512:(qc + 1) * 512], start=True, stop=True),
                                 reads=[KT_, QT], writes=[ps_])
                            E = nextE()
                            act(E[:], ps_[:], AF.Exp, [ps_], [E], scale=SCALE)
                            d = kt - 4 * qc
                            if br == 0:
                                pm = nps()
                                k.op('pe', lambda e: e.matmul(pm[:, :], lhsT=ExAll[:, kt, :], rhs=selT[:, qc * 512:(qc + 1) * 512], start=True, stop=True),
                                     reads=[ExAll, selT], writes=[pm])
                                tt('dve', E[:], E[:], pm[:], ALU.mult, [E, pm], [E])
                                if d >= 0:
                                    tt('pool', E[:], E[:], causD[:, d, :], ALU.mult, [E, causD], [E])
                            else:
                                tt('pool', E[:], E[:], winD[:, d + 4, :], ALU.mult, [E, winD], [E])
                            for sub in range(4):
                                k.op('pe', lambda e: e.matmul(ACC[sub][:, 0:129], lhsT=E[:, sub * 128:(sub + 1) * 128], rhs=V_[:, kt, :],
                                                              start=(ki == 0), stop=(ki == len(kts) - 1)), reads=[E, V_], writes=[ACC[sub]])
                        for sub in range(4):
                            ti = qc * 4 + sub
                            finish_acc(ACC[sub], 128, 128, oacc[:, ti, r, :], gs[:, ti, 3 * r + 1 + br:3 * r + 2 + br], False)

        if do_sample:
            nsa_sample(k, C_, hh, gl, T, locals())

        for ti, (t0, nt) in enumerate(TT):
            pa = nps()
            k.ops('pe', [(lambda e, j=j: e.transpose(pa[:, j * 128:j * 128 + nt], oacc[0:nt, ti, j, :], ident[0:nt, 0:nt])) for j in range(4)],
                  reads=[oacc, ident], writes=[pa])
            act(oTs[:, :, t0:t0 + nt], pa[:, :].rearrange('p (r t) -> p r t', t=128)[:, :, 0:nt], AF.Copy, [pa], [oTs])
        for r in range(4):
            k.dma('sp', oT_all, oT_all.t[hh * 16 + 8 + 4 * gl + r], oTs, oTs[:, r, :])
    k.end_scope(sc)


def nsa_sample(k, C_, hh, gl, T, L):
    (ACC, nps, act, tt, ts, QT, KsT, KwT, Vs, Vw, gs, oacc, kcn, phi, wpool, ovS, keepS, biasS, ExS, maskWS, maskNs, idx,
     finish_acc, rms_cols, gk0c, ident, m8) = (L[n] for n in (
        'ACC', 'nps', 'act', 'tt', 'ts', 'QT', 'KsT', 'KwT', 'Vs', 'Vw', 'gs', 'oacc', 'kcn', 'phi', 'wpool', 'ovS', 'keepS', 'biasS', 'ExS',
        'maskWS', 'maskN', 'idx', 'finish_acc', 'rms_cols', 'gk0c', 'ident', 'm8'))
    nseq = L['nseq']
    g_abs = 2 * hh + gl
    cache_cmp, slcKT, slcV, winKT, winV = (T[n] for n in ('cache_cmp', 'slcKT', 'slcV', 'winKT', 'winV'))
    sc = k.scope()
    cpg = [k.sbuf('cpg%d' % i, [128, 2, 4, 128], BF16) for i in range(3)]
    ktp = [k.sbuf('ktp%d' % i, [128, 4, 128], BF16) for i in range(3)]
    vtp = [k.sbuf('vtp%d' % i, [128, 4, 128], BF16) for i in range(3)]
    vaug = [k.sbuf('vaug%d' % i, [128, 129], BF16) for i in range(3)]
    poolS = k.sbuf('poolS', [128, 2, 2, 64, 8], F32)
    pooledS = k.sbuf('pooledS', [128, 2, 512], BF16)
    vcS = k.sbuf('vcS', [128, 4, 258], BF16)
    E32c = k.sbuf('E32c', [128, 4, 32], BF16)
    E32s = [k.sbuf('E32s%d' % i, [128, 4, 32], BF16) for i in range(3)]
    E32w = k.sbuf('E32w', [128, 4, 4, 32], BF16)
    EnR = k.sbuf('EnR', [32, 2, 4, 32], BF16)
    EnS = [k.sbuf('EnS%d' % i, [32, 32], BF16) for i in range(2)]
    impS = k.sbuf('impS', [32, 129], F32)
    selS = k.sbuf('selS', [32, 129], F32)
    sct = k.sbuf('sct', [32, 129], F32)
    selTS = k.sbuf('selTS', [128, 32], BF16)
    wkt = k.sbuf('wkt', [128, 512], BF16)
    Mps = [k.sbuf('Mps%d' % i, [128, 32], BF16) for i in range(3)]
    wv = k.sbuf('wv', [128, 4, 129], BF16)
    for b_ in vaug:
        k.op('dve', lambda e: e.memset(b_[:, 128:129], 1.0), writes=[b_])
    k.op('dve', lambda e: e.memset(wv[:, :, 128:129], 1.0), writes=[wv])
    k.op('dve', lambda e: e.memset(vcS[:, :, 257:258], 1.0), writes=[vcS])
    k.op('dve', lambda e: e.tensor_copy(out=vcS[:, :, 128:257], in_=ovS[:]), reads=[ovS], writes=[vcS])
    k.op('dve', lambda e: e.memset(oacc[0:32, 16, :, :], 0.0), writes=[oacc])
    qs = lambda r, s: QT[:, r, SEQ + 4 * s:SEQ + 4 * s + 4]
    for br, KT_ in ((0, KsT), (1, KwT)):
        pn = nps()
        k.ops('pe', [(lambda e, r=r: e.matmul(pn[0:32, r * 32:(r + 1) * 32], lhsT=KT_[:, SEQ:TOK], rhs=QT[:, r, SEQ:TOK], start=True, stop=True)) for r in range(4)],
              reads=[KT_, QT], writes=[pn])
        act(EnR[:, br, :, :], pn[0:32, 0:128].rearrange('p (r c) -> p r c', c=32), AF.Exp, [pn], [EnR], scale=SCALE)

    def gather(dst_buf, dst_ap, src_buf, col):
        k._wait('pool', list(idx.last_w.items()) + list(dst_buf.last_w.items()) + list(dst_buf.readers.items()))
        sk = k._dsem(dst_buf, 'pool')
        ins = k.nc.gpsimd.indirect_dma_start(out=dst_ap, out_offset=None, in_=src_buf.t,
                                             in_offset=bass.IndirectOffsetOnAxis(ap=idx[:, col:col + 1], axis=0))
        k.dcount[sk] += 1
        ins.then_inc(k.sems[sk], 16)
        k.n_inst += 1
        dst_buf.last_w = {sk: 16 * k.dcount[sk]}
        dst_buf.readers = {}
        idx.readers[sk] = 16 * k.dcount[sk]

    gi = 0
    for s in range(nseq):
        for q4 in range(4):
            pc = nps()
            for pp in range(16):
                p = q4 * 16 + pp
                cb = cpg[gi % 3]; gi += 1
                gather(cb, cb[:].rearrange('p a g d -> p (a g d)'), cache_cmp, s * 64 + p)
                k.ops('pe', [(lambda e, kv=kv: e.matmul(pc[:, (pp * 2 + kv) * 16:(pp * 2 + kv + 1) * 16], lhsT=cb[:, kv, g_abs, :], rhs=wpool[:, gl, kv, :],
                                                        start=True, stop=True)) for kv in range(2)], reads=[cb, wpool], writes=[pc])
            for kv in range(2):
                src = pc[:, :].rearrange('p (pg kv c j) -> p pg kv c j', kv=2, c=2, j=8)[:, :, kv, :, :]
                dst = poolS[:, kv, :, q4 * 16:(q4 + 1) * 16, :].rearrange('p c pg j -> p pg c j')
                act(dst, src, AF.Copy, [pc], [poolS])
        for kv in range(2):
            a0 = poolS[:, kv, 0].rearrange('p pg j -> p (pg j)')
            a1 = poolS[:, kv, 1].rearrange('p pg j -> p (pg j)')
            tt('dve', pooledS[:, kv, 0:511], a0[:, 0:511], a1[:, 1:512], ALU.add, [poolS], [pooledS])
        pkc = nps()
        k.op('pe', lambda e: e.matmul(pkc[:, 0:511], lhsT=phi[:, gl, 0, :], rhs=pooledS[:, 0, 0:511], start=True, stop=True), reads=[phi, pooledS], writes=[pkc])
        rms_cols(pkc, 511, gk0c[:, 0:1], kcn[:, 0:511])
        for nt in range(4):
            nn = 128 if nt < 3 else 127
            pvc = nps()
            k.op('pe', lambda e: e.matmul(pvc[0:nn, 0:128], lhsT=pooledS[:, 1, nt * 128:nt * 128 + nn], rhs=phi[:, gl, 1, :], start=True, stop=True),
                 reads=[phi, pooledS], writes=[pvc])
            act(vcS[0:nn, nt, 0:128], pvc[0:nn, 0:128], AF.Copy, [pvc], [vcS])
        for r in range(4):
            k.op('dve', lambda e: e.memset(E32c[:], 0.0), writes=[E32c])
            pss = nps()
            k.ops('pe', [(lambda e, nt=nt: e.matmul(pss[0:(128 if nt < 3 else 127), nt * 4:nt * 4 + 4], lhsT=kcn[:, nt * 128:nt * 128 + (128 if nt < 3 else 127)],
                                                    rhs=qs(r, s), start=True, stop=True)) for nt in range(4)], reads=[kcn, QT], writes=[pss])
            act(E32c[:, :, 4 * s:4 * s + 4], pss[:, 0:16].rearrange('p (n t) -> p n t', t=4), AF.Exp, [pss], [E32c], scale=SCALE)
            pa_ = ACC[r]
            k.ops('pe', [(lambda e, nt=nt: e.matmul(pa_[0:32, 0:258], lhsT=E32c[0:(128 if nt < 3 else 127), nt, :], rhs=vcS[0:(128 if nt < 3 else 127), nt, :],
                                                    start=(nt == 0), stop=(nt == 3))) for nt in range(4)], reads=[E32c, vcS], writes=[pa_])
            r_ = finish_acc(pa_, 257, 32, oacc[0:32, 16, r, :], gs[0:32, 16, 3 * r:3 * r + 1], False)
            if r == 0:
                ts('dve', impS[:], pa_[0:32, 128:257], r_[0:32, 0:1], ALU.mult, [pa_, r_], [impS])
            else:
                k.op('dve', lambda e: e.scalar_tensor_tensor(out=impS[:], in0=pa_[0:32, 128:257], scalar=r_[0:32, 0:1], in1=impS[:], op0=ALU.mult, op1=ALU.add),
                     reads=[pa_, r_, impS], writes=[impS])
        tt('dve', impS[:], impS[:], keepS[:], ALU.mult, [impS, keepS], [impS])
        tt('dve', impS[:], impS[:], biasS[:], ALU.add, [impS, biasS], [impS])
        k.op('dve', lambda e: e.max(out=m8[0:32, 0:8], in_=impS[:]), reads=[impS], writes=[m8])
        k.op('dve', lambda e: e.match_replace(out=sct[:], in_to_replace=m8[0:32, 0:8], in_values=impS[:], imm_value=-3.0e38), reads=[m8, impS], writes=[sct])
        k.op('dve', lambda e: e.max(out=m8[0:32, 8:16], in_=sct[:]), reads=[sct], writes=[m8])
        ts('dve', selS[:], impS[:], m8[0:32, 15:16], ALU.is_ge, [impS, m8], [selS])
        pt_ = nps()
        k.op('pe', lambda e: e.transpose(pt_[:, 0:32], selS[:, 0:128], ident[0:32, 0:32]), reads=[selS, ident], writes=[pt_])
        act(selTS[:], pt_[:, 0:32], AF.Copy, [pt_], [selTS])
        for b_ in E32s:
            k.op('dve', lambda e: e.memset(b_[:], 0.0), writes=[b_])
        for p in range(64):
            kb = ktp[p % 3]; vb_ = vtp[p % 3]; Eb_ = E32s[p % 3]
            gather(kb, kb[:].rearrange('p g t -> p (g t)'), slcKT, s * 64 + p)
            gather(vb_, vb_[:].rearrange('p g d -> p (g d)'), slcV, s * 64 + p)
            va_ = vaug[p % 3]
            k.op('pool', lambda e: e.tensor_copy(out=va_[:, 0:128], in_=vb_[:, g_abs, :]), reads=[vb_], writes=[va_])
            pm = nps()
            k.op('pe', lambda e: e.matmul(pm[:, 0:32], lhsT=ExS[:, p, :], rhs=selTS[:], start=True, stop=True), reads=[ExS, selTS], writes=[pm])
            pss = nps()
            k.ops('pe', [(lambda e, r=r: e.matmul(pss[:, r * 4:r * 4 + 4], lhsT=kb[:, g_abs, :], rhs=qs(r, s), start=True, stop=True)) for r in range(4)],
                  reads=[kb, QT], writes=[pss])
            act(Eb_[:, :, 4 * s:4 * s + 4], pss[:, 0:16].rearrange('p (r t) -> p r t', t=4), AF.Exp, [pss], [Eb_], scale=SCALE)
            mps = Mps[p % 3]
            act(mps[:], pm[:, 0:32], AF.Copy, [pm], [mps])
            tt('dve', Eb_[:], Eb_[:], bc_mid(mps[:], 4), ALU.mult, [Eb_, mps], [Eb_])
            for r in range(4):
                k.op('pe', lambda e: e.matmul(ACC[r][0:32, 0:129], lhsT=Eb_[:, r, :], rhs=va_[:, :], start=(p == 0), stop=False), reads=[Eb_, va_], writes=[ACC[r]])
        for r in range(4):
            en = EnS[r % 2]
            tt('pool', en[:], EnR[:, 0, r, :], maskNs[:, s, :], ALU.mult, [EnR, maskNs], [en])
            k.op('pe', lambda e: e.matmul(ACC[r][0:32, 0:129], lhsT=en[:], rhs=Vs[0:32, 16, :], start=False, stop=True), reads=[en, Vs], writes=[ACC[r]])
            finish_acc(ACC[r], 128, 32, oacc[0:32, 16, r, :], gs[0:32, 16, 3 * r + 1:3 * r + 2], False)
        k.dma('pool', wkt, wkt[:], winKT, winKT.t[hh, s, gl])
        k.dma('pool', wv, wv[:, :, 0:128], winV, winV.t[hh, s, gl].rearrange('(kt p) d -> p kt d', p=128))
        k.op('dve', lambda e: e.memset(E32w[:], 0.0), writes=[E32w])
        pss = nps()
        k.ops('pe', [(lambda e, kt=kt, r=r: e.matmul(pss[:, (kt * 4 + r) * 4:(kt * 4 + r) * 4 + 4], lhsT=wkt[:, kt * 128:(kt + 1) * 128], rhs=qs(r, s), start=True, stop=True))
                     for kt in range(4) for r in range(4)], reads=[wkt, QT], writes=[pss])
        act(E32w[:, :, :, 4 * s:4 * s + 4], pss[:, 0:64].rearrange('p (a r t) -> p a r t', r=4, t=4), AF.Exp, [pss], [E32w], scale=SCALE)
        for kt in range(4):
            tt('dve', E32w[:, kt], E32w[:, kt], bc_mid(maskWS[:, kt, :], 4), ALU.mult, [E32w, maskWS], [E32w])
        for r in range(4):
            k.ops('pe', [(lambda e, kt=kt: e.matmul(ACC[r][0:32, 0:129], lhsT=E32w[:, kt, r, :], rhs=wv[:, kt, :], start=(kt == 0), stop=False)) for kt in range(4)],
                  reads=[E32w, wv], writes=[ACC[r]])
            en = EnS[r % 2]
            tt('pool', en[:], EnR[:, 1, r, :], maskNs[:, s, :], ALU.mult, [EnR, maskNs], [en])
            k.op('pe', lambda e: e.matmul(ACC[r][0:32, 0:129], lhsT=en[:], rhs=Vw[0:32, 16, :], start=False, stop=True), reads=[en, Vw], writes=[ACC[r]])
            finish_acc(ACC[r], 128, 32, oacc[0:32, 16, r, :], gs[0:32, 16, 3 * r + 2:3 * r + 3], False)
    k.end_scope(sc)

NOWN = 1040
D_FF = 16384


def phase_c(k, C_, oT_all, T, o_y, passes=((0, 512), (512, 528)), nfc=64):
    ones_bf, eps_t = C_['ones_bf'], C_['eps']
    x_own, w_o, w_up, w_down, g_mlp, hsel = (T[n] for n in ('x_own', 'w_o_p', 'w_up', 'w_down', 'g_mlp', 'hsel'))
    sc = k.scope()
    PS = [k.psum('cp%d' % i, [128, 1024], F32) for i in range(4)]
    pi = [0]

    def nps():
        p = PS[pi[0] % 4]
        pi[0] += 1
        return p

    def act(out, in_, func, r, w, **kw):
        k.op('act', lambda e: e.activation(out=out, in_=in_, func=func, **kw), reads=r, writes=w)

    hsel_s = k.sbuf('hsel_s', [128, 2], F32)
    k.dma('sp', hsel_s, hsel_s[:], hsel, hsel.t)
    gm_s = k.sbuf('gm_s', [128, KT], F32)
    k.dma('sp', gm_s, gm_s[:], g_mlp, g_mlp.t)
    yacc = k.sbuf('yacc', [128, KT, 528], F32)

    def mm_cols(p, n):
        return [(0, min(n, 512), 0)] + ([(512, n - 512, 512)] if n > 512 else [])

    for (c0, n) in passes:
        s1 = k.scope()
        oOwn = k.sbuf('oOwn', [128, KT, 528], BF16)
        ofull = [k.sbuf('ofull%d' % i, [128, TOK], BF16) for i in range(2)]
        otmp = k.sbuf('otmp', [128, 528], BF16)
        wob = [k.sbuf('wob%d' % i, [128, KT, 512], BF16) for i in range(2)]
        xs = [k.sbuf('xs%d' % i, [128, 528], F32) for i in range(2)]
        npc = min(n, 512)
        for ft in range(KT):
            of = ofull[ft % 2]
            k.dma('sp', of, of[:], oT_all, oT_all.t[ft])
            k.op('dve', lambda e: e.tensor_scalar(out=otmp[:, 0:npc], in0=of[:, c0:c0 + npc], scalar1=hsel_s[:, 0:1], scalar2=None, op0=ALU.mult),
                 reads=[of, hsel_s], writes=[otmp])
            k.op('dve', lambda e: e.scalar_tensor_tensor(out=oOwn[:, ft, 0:npc], in0=of[:, 1024 + c0:1024 + c0 + npc], scalar=hsel_s[:, 1:2], in1=otmp[:, 0:npc],
                                                          op0=ALU.mult, op1=ALU.add), reads=[of, hsel_s, otmp], writes=[oOwn])
            if n > 512:
                k.op('dve', lambda e: e.tensor_scalar(out=otmp[:, 512:528], in0=of[:, SEQ:SEQ + 16], scalar1=hsel_s[:, 0:1], scalar2=None, op0=ALU.mult),
                     reads=[of, hsel_s], writes=[otmp])
                k.op('dve', lambda e: e.scalar_tensor_tensor(out=oOwn[:, ft, 512:528], in0=of[:, SEQ + 16:SEQ + 32], scalar=hsel_s[:, 1:2], in1=otmp[:, 512:528],
                                                              op0=ALU.mult, op1=ALU.add), reads=[of, hsel_s, otmp], writes=[oOwn])
        for db in range(8):
            wb = wob[db % 2]
            v = w_o.t[:, db * 512:(db + 1) * 512].rearrange('(kt p) n -> p kt n', p=128)
            for hq in range(4):
                k.dma('pool', wb, wb[:, hq * 8:(hq + 1) * 8, :], w_o, v[:, hq * 8:(hq + 1) * 8, :], par=True)
            for dj in range(4):
                dm = db * 4 + dj
                x_ = xs[dm % 2]
                k.dma('sp', x_, x_[:, 0:n], x_own, x_own.t[dm * 128:(dm + 1) * 128, c0:c0 + n])
                p = nps()
                fns = []
                for kt in range(KT):
                    for (a0, an, pa) in mm_cols(p, n):
                        fns.append(lambda e, kt=kt, a0=a0, an=an, pa=pa: e.matmul(p[:, pa:pa + an], lhsT=wb[:, kt, dj * 128:(dj + 1) * 128], rhs=oOwn[:, kt, a0:a0 + an],
                                                                                   start=(kt == 0), stop=(kt == KT - 1)))
                k.ops('pe', fns, reads=[wb, oOwn], writes=[p])
                for (a0, an, pa) in mm_cols(p, n):
                    k.op('dve', lambda e: e.tensor_tensor(out=yacc[:, dm, a0:a0 + an], in0=p[:, pa:pa + an], in1=x_[:, a0:a0 + an], op=ALU.add),
                         reads=[p, x_], writes=[yacc])
        k.end_scope(s1)
        sP = k.scope()
        h2T = k.sbuf('h2T', [128, KT, 528], BF16)
        s2 = k.scope()
        sq = [k.sbuf('csq%d' % i, [128, 528], BF16) for i in range(2)]
        rstd = k.sbuf('crstd', [128, 528], F32)
        pss = nps()
        for dm in range(KT):
            s_ = sq[dm % 2]
            act(s_[:, 0:n], yacc[:, dm, 0:n], AF.Square, [yacc], [s_])
            for (a0, an, pa) in mm_cols(pss, n):
                k.op('pe', lambda e: e.matmul(pss[:, pa:pa + an], lhsT=ones_bf[:], rhs=s_[:, a0:a0 + an], start=(dm == 0), stop=(dm == KT - 1)),
                     reads=[ones_bf, s_], writes=[pss])
        for (a0, an, pa) in mm_cols(pss, n):
            act(rstd[:, a0:a0 + an], pss[:, pa:pa + an], AF.Ln, [pss, eps_t], [rstd], bias=eps_t[:, :], scale=1.0 / D_MODEL)
        act(rstd[:, 0:n], rstd[:, 0:n], AF.Exp, [rstd], [rstd], scale=-0.5)
        for dm in range(KT):
            k.op('dve', lambda e: e.scalar_tensor_tensor(out=h2T[:, dm, 0:n], in0=yacc[:, dm, 0:n], scalar=gm_s[:, dm:dm + 1], in1=rstd[:, 0:n],
                                                          op0=ALU.mult, op1=ALU.mult), reads=[yacc, gm_s, rstd], writes=[h2T])
        k.end_scope(s2)
        s3 = k.scope()
        wub = [k.sbuf('wub%d' % i, [128, KT, 256], BF16) for i in range(2)]
        wdb = [k.sbuf('wdb%d' % i, [128, 2, D_MODEL], BF16) for i in range(2)]
        hid = [k.sbuf('hid%d' % i, [128, 2, 528], BF16) for i in range(2)]
        rl = [k.sbuf('rl%d' % i, [128, 528], F32) for i in range(2)]
        for fc in range(nfc):
            wu = wub[fc % 2]; wd = wdb[fc % 2]; hd_ = hid[fc % 2]
            vu = w_up.t[:, fc * 256:(fc + 1) * 256].rearrange('(kt p) n -> p kt n', p=128)
            for hq in range(4):
                k.dma('pool', wu, wu[:, hq * 8:(hq + 1) * 8, :], w_up, vu[:, hq * 8:(hq + 1) * 8, :], par=True)
            vd = w_down.t[fc * 256:(fc + 1) * 256, :].rearrange('(ft p) n -> p ft n', p=128)
            for fq in range(2):
                k.dma('pool', wd, wd[:, fq, :], w_down, vd[:, fq, :], par=True)
            for ft in range(2):
                p = nps()
                fns = []
                for kt in range(KT):
                    for (a0, an, pa) in mm_cols(p, n):
                        fns.append(lambda e, kt=kt, a0=a0, an=an, pa=pa: e.matmul(p[:, pa:pa + an], lhsT=wu[:, kt, ft * 128:(ft + 1) * 128], rhs=h2T[:, kt, a0:a0 + an],
                                                                                   start=(kt == 0), stop=(kt == KT - 1)))
                k.ops('pe', fns, reads=[wu, h2T], writes=[p])
                r_ = rl[ft % 2]
                for (a0, an, pa) in mm_cols(p, n):
                    act(r_[:, a0:a0 + an], p[:, pa:pa + an], AF.Relu, [p], [r_])
                k.op('pool', lambda e: e.tensor_tensor(out=hd_[:, ft, 0:n], in0=r_[:, 0:n], in1=r_[:, 0:n], op=ALU.mult), reads=[r_], writes=[hd_])
            for dm in range(KT):
                p = nps()
                fns = []
                for ft in range(2):
                    for (a0, an, pa) in mm_cols(p, n):
                        fns.append(lambda e, ft=ft, a0=a0, an=an, pa=pa: e.matmul(p[:, pa:pa + an], lhsT=wd[:, ft, dm * 128:(dm + 1) * 128], rhs=hd_[:, ft, a0:a0 + an],
                                                                                   start=(ft == 0), stop=(ft == 1)))
                k.ops('pe', fns, reads=[wd, hd_], writes=[p])
                for (a0, an, pa) in mm_cols(p, n):
                    k.op('dve', lambda e: e.tensor_tensor(out=yacc[:, dm, a0:a0 + an], in0=p[:, pa:pa + an], in1=yacc[:, dm, a0:a0 + an], op=ALU.add),
                         reads=[p, yacc], writes=[yacc])
        k.end_scope(s3)
        k.end_scope(sP)
        for dm in range(KT):
            k.dma('sp', o_y, o_y.t[dm * 128:(dm + 1) * 128, c0:c0 + n], yacc, yacc[:, dm, 0:n])
    k.end_scope(sc)

from concourse.bass_utils import run_bass_kernel_spmd

D_MODEL = 4096
KT = 32
SEQ = 2048
NS = 8
TS = 4
TOK = SEQ + NS * TS
HD = 128
NH = 8
EPS = 1e-6
CH = [(0, 512), (512, 512), (1024, 512), (1536, 512), (2048, 32)]
TT = [(i * 128, 128) for i in range(16)] + [(2048, 32)]
NTM = 2600


def dview(buf, pat, **kw):
    return buf.t.rearrange(pat, **kw)


def build(stage=9, cut=99, nheads=NH, nsa_kw=None, c_kw=None):
    k = K()
    nc = k.nc
    nsa_kw = nsa_kw or {}
    c_kw = c_kw or {}
    def din(name, shape, dt=F32):
        return k.dram(name, shape, dt, kind='ExternalInput')

    def dout(name, shape, dt=F32):
        return k.dram(name, shape, dt, kind='ExternalOutput')

    xT = din('xT', [D_MODEL, TOK])
    g_attn = din('g_attn', [128, KT])
    gq_b = din('gq_b', [128, 128])
    gk_b = din('gk_b', [128, 3, 128])
    H = []
    outs = []
    for hh in range(2):
        d = {}
        d['w_fm'] = din('w_fm%d' % hh, [D_MODEL, NH * 512])
        d['w_tm'] = din('w_tm%d' % hh, [D_MODEL, NTM])
        d['win_cache'] = din('win_cache%d' % hh, [NS, 512, 512])
        d['convc'] = din('convc%d' % hh, [128, NH, 3, NS, 3])
        d['convw'] = din('convw%d' % hh, [128, NH, 3, 4])
        d['alog_b'] = din('alog_b%d' % hh, [128, NH])
        d['dtb_b'] = din('dtb_b%d' % hh, [128, NH])
        d['state_in'] = din('state_in%d' % hh, [NS, NH, 128, 128])
        d['o_cmp'] = dout('o_cmp%d' % hh, [TOK, 512])
        d['o_slc'] = dout('o_slc%d' % hh, [TOK, 512])
        d['o_win'] = dout('o_win%d' % hh, [TOK, 512])
        d['o_winS'] = dout('o_winS%d' % hh, [NS, 512, 512])
        d['o_conv'] = dout('o_conv%d' % hh, [27, NH, 384])
        d['o_stP'] = dout('o_stP%d' % hh, [NH, 128, 128])
        d['o_stS'] = dout('o_stS%d' % hh, [NS, NH, 128, 128])
        d['featT'] = k.dram('featT%d' % hh, [NH, 4, 128, TOK], F32)
        d['qn'] = k.dram('qn%d' % hh, [TOK, 1024], F32)
        d['small'] = k.dram('small%d' % hh, [TOK, 40], F32)
        outs += [d[n] for n in ('o_cmp', 'o_slc', 'o_win', 'o_winS', 'o_conv', 'o_stP', 'o_stS')]
        H.append(d)
    gnorm = din('gnorm', [128, 1])
    c_pack = din('c_pack', [128, 7, 128])
    import os as _os
    oT_all = k.dram('oT_all', [32, 128, TOK], BF16, kind=('ExternalOutput' if _os.environ.get('DEBUG_OT') else 'Internal'))
    o_y = dout('o_y', [D_MODEL, 1040])
    outs.append(o_y)
    NT = {}
    for nm, shp, dt in (('maskC', [128, SEQ], F32), ('causD', [128, 4, 512], F32), ('winD', [128, 8, 512], F32), ('ExAll', [32, 16, 128], F32),
                        ('ovP', [128, 32], F32), ('keepP', [128, 16, 32], F32), ('biasP', [128, 16, 32], F32), ('gk0c', [128, 1], F32),
                        ('wpool', [2, 128, 2, 2, 16], F32), ('phi', [2, 128, 2, 2, 128], F32), ('ovS', [128, 4, 129], F32),
                        ('keepS', [32, 129], F32), ('biasS', [32, 129], F32), ('ExS', [128, 64, 128], F32), ('maskWS', [128, 4, 32], F32),
                        ('maskNs', [32, 8, 32], F32), ('iota_p', [128, 1], F32), ('page_tab', [1, NS * 64], I32),
                        ('cache_cmp', [2560 * 128, 1024], F32), ('slcKT', [2560 * 128, 512], F32), ('slcV', [2560 * 128, 512], F32),
                        ('winKT', [2, NS, 2, 128, 512], F32), ('winV', [2, NS, 2, 512, 128], F32),
                        ('x_own', [D_MODEL, 1040], F32), ('w_o_p', [D_MODEL, D_MODEL], F32), ('w_up', [D_MODEL, 16384], F32),
                        ('w_down', [16384, D_MODEL], F32), ('g_mlp', [128, KT], F32), ('hsel', [128, 2], F32)):
        NT[nm] = din(nm, shp, dt)
    NT['oT_all'] = oT_all

    ones_bf = k.sbuf('ones_bf', [128, 128], BF16)
    k.op('dve', lambda e: e.memset(ones_bf[:], 1.0), writes=[ones_bf])
    eps_t = k.sbuf('eps_t', [128, 1], F32)
    k.op('dve', lambda e: e.memset(eps_t[:], EPS), writes=[eps_t])
    scH = k.scope()
    g_attn_s = k.sbuf('g_attn_s', [128, KT], F32)
    k.dma('sp', g_attn_s, g_attn_s[:], g_attn, g_attn.t)
    gq_s = k.sbuf('gq_s', [128, 128], F32)
    k.dma('sp', gq_s, gq_s[:], gq_b, gq_b.t)
    gk_s = k.sbuf('gk_s', [128, 3, 128], F32)
    k.dma('sp', gk_s, gk_s[:], gk_b, gk_b.t)
    hT = k.sbuf('hT', [128, KT, TOK], BF16)
    def rstd_from(e_in, out_ap, scale, bufs_r, bufs_w):
        k.op('act', lambda e: e.activation(out=out_ap, in_=e_in, func=AF.Ln, bias=eps_t[0:out_ap.shape[0], :], scale=scale),
             reads=bufs_r + [eps_t], writes=bufs_w)
        k.op('act', lambda e: e.activation(out=out_ap, in_=out_ap, func=AF.Exp, scale=-0.5),
             reads=bufs_w, writes=bufs_w)

    sc0 = k.scope()
    xb = [k.sbuf('xb%d' % i, [128, TOK], F32) for i in range(2)]
    sq = [k.sbuf('sq%d' % i, [128, TOK], BF16) for i in range(2)]
    rstd = k.sbuf('rstd', [128, TOK], F32)
    ssp = [k.psum('ssp%d' % i, [128, 512], F32) for i in range(5)]
    for kt in range(KT):
        x_ = xb[kt % 2]; s_ = sq[kt % 2]
        k.dma('sp', x_, x_[:], xT, xT.t[kt * 128:(kt + 1) * 128, :])
        k.op('act', lambda e: e.activation(out=s_[:], in_=x_[:], func=AF.Square), reads=[x_], writes=[s_])
        for ci, (c0, n) in enumerate(CH):
            k.op('pe', lambda e: e.matmul(ssp[ci][:, :n], lhsT=ones_bf[:], rhs=s_[:, c0:c0 + n], start=(kt == 0), stop=(kt == KT - 1)),
                 reads=[ones_bf, s_], writes=[ssp[ci]])
    for ci, (c0, n) in enumerate(CH):
        rstd_from(ssp[ci][:, :n], rstd[:, c0:c0 + n], 1.0 / D_MODEL, [ssp[ci]], [rstd])
    for kt in range(KT):
        x_ = xb[kt % 2]
        k.dma('sp', x_, x_[:], xT, xT.t[kt * 128:(kt + 1) * 128, :])
        k.op('dve', lambda e: e.scalar_tensor_tensor(out=hT[:, kt, :], in0=x_[:], scalar=g_attn_s[:, kt:kt + 1], in1=rstd[:],
                                                      op0=ALU.mult, op1=ALU.mult),
             reads=[x_, g_attn_s, rstd], writes=[hT])
    k.end_scope(sc0)
    for hh in range(2):
        d = H[hh]
        w_fm, w_tm, win_cache, o_cmp, o_slc, o_win, o_winS, o_conv, featT, qn, small = (d[n] for n in ('w_fm', 'w_tm', 'win_cache', 'o_cmp', 'o_slc', 'o_win', 'o_winS', 'o_conv', 'featT', 'qn', 'small'))
        scA = k.scope()
        hsel = k.sbuf('hsel', [128, KT, 32], BF16)
        k.op('dve', lambda e: e.memset(hsel[:], 0.0), writes=[hsel])
        k.op('dve', lambda e: e.tensor_copy(out=hsel[:, :, 0:3], in_=hT[:, :, SEQ - 3:SEQ]), reads=[hT], writes=[hsel])
        for s in range(NS):
            k.op('dve', lambda e: e.tensor_copy(out=hsel[:, :, 3 + 3 * s:6 + 3 * s], in_=hT[:, :, SEQ + 4 * s + 1:SEQ + 4 * s + 4]),
                 reads=[hT], writes=[hsel])
        wblk = [k.sbuf('wblk%d' % i, [128, KT, 512], BF16) for i in range(2)]
        tst = [k.sbuf('tst%d' % i, [128, 512], F32) for i in range(3)]
        tsi = [0]
        pp = [k.psum('pp%d' % i, [128, 512], F32) for i in range(6)]
        pi = [0]

        def next_ps():
            p = pp[pi[0] % len(pp)]
            pi[0] += 1
            return p

        def load_w(wb, src, c0, ncol):
            v = src.t[:, c0:c0 + ncol].rearrange('(kt p) n -> p kt n', p=128)
            for hh in range(4):
                k.dma('pool', wb, wb[:, hh * 8:(hh + 1) * 8, :ncol], src, v[:, hh * 8:(hh + 1) * 8, :], par=True)

        ev = [0]

        def evac(out_ap, in_ap, r, w):
            ev[0] += 1
            if ev[0] % 2:
                k.op('act', lambda e: e.activation(out=out_ap, in_=in_ap, func=AF.Copy), reads=r, writes=w)
            else:
                k.op('dve', lambda e: e.tensor_copy(out=out_ap, in_=in_ap), reads=r, writes=w)

        if stage >= 1:
            for hd in range(NH):
                wb = wblk[hd % 2]
                load_w(wb, w_fm, hd * 512, 512)
                for ct in range(4):
                    for (c0, n) in CH:
                        p = next_ps()
                        st = tst[tsi[0] % 3]; tsi[0] += 1
                        k.ops('pe', [(lambda e, kt=kt: e.matmul(p[:, :n], lhsT=wb[:, kt, ct * 128:(ct + 1) * 128], rhs=hT[:, kt, c0:c0 + n],
                                                                start=(kt == 0), stop=(kt == KT - 1))) for kt in range(KT)],
                              reads=[wb, hT], writes=[p])
                        evac(st[:, :n], p[:, :n], [p], [st])
                        k.dma('sp', featT, featT.t[hd, ct, :, c0:c0 + n], st, st[:, :n], par=True)
                p = next_ps()
                k.ops('pe', [(lambda e, kt=kt: e.matmul(p[0:27, 0:384], lhsT=hsel[:, kt, 0:27], rhs=wb[:, kt, 0:384],
                                                        start=(kt == 0), stop=(kt == KT - 1))) for kt in range(KT)],
                      reads=[wb, hsel], writes=[p])
                cs = tst[tsi[0] % 3]; tsi[0] += 1
                evac(cs[0:27, 0:384], p[0:27, 0:384], [p], [cs])
                k.dma('sp', o_conv, o_conv.t[:, hd, :], cs, cs[0:27, 0:384], par=True)

        if stage >= 1:
            junk = k.sbuf('junk', [128, 128], F32)
            ssq = [k.sbuf('ssq%d' % i, [128, 4], F32) for i in range(2)]
            blocks = [('nq', 0, 512), ('nq', 512, 512), ('cmp', 1024, 512), ('slc', 1536, 512), ('win', 2048, 512), ('small', 2560, 40)]
            ti = 0
            for bi, (kind, c0, ncol) in enumerate(blocks):
                wb = wblk[bi % 2]
                load_w(wb, w_tm, c0, ncol)
                for (t0, nt) in TT:
                    p = next_ps()
                    k.ops('pe', [(lambda e, kt=kt: e.matmul(p[0:nt, 0:ncol], lhsT=hT[:, kt, t0:t0 + nt], rhs=wb[:, kt, 0:ncol],
                                                            start=(kt == 0), stop=(kt == KT - 1))) for kt in range(KT)],
                          reads=[wb, hT], writes=[p])
                    ts_ = tst[tsi[0] % 3]; tsi[0] += 1; sq_ = ssq[ti % 2]; ti += 1
                    nnorm = {'nq': 4, 'slc': 2, 'win': 2}.get(kind, 0)
                    if nnorm:
                        for j in range(nnorm):
                            k.op('act', lambda e: e.activation(out=junk[0:nt, :], in_=p[0:nt, j * 128:(j + 1) * 128], func=AF.Square,
                                                               accum_out=sq_[0:nt, j:j + 1]), reads=[p], writes=[junk, sq_])
                        rstd_from(sq_[0:nt, 0:nnorm], sq_[0:nt, 0:nnorm], 1.0 / HD, [sq_], [sq_])
                        gvec = gq_s[0:nt, :] if kind == 'nq' else gk_s[0:nt, 1 if kind == 'slc' else 2, :]
                        gbuf = gq_s if kind == 'nq' else gk_s
                        for j in range(nnorm):
                            k.op('dve', lambda e: e.scalar_tensor_tensor(out=ts_[0:nt, j * 128:(j + 1) * 128], in0=p[0:nt, j * 128:(j + 1) * 128],
                                                                          scalar=sq_[0:nt, j:j + 1], in1=gvec, op0=ALU.mult, op1=ALU.mult),
                                 reads=[p, sq_, gbuf], writes=[ts_])
                        if nnorm * 128 < ncol:
                            evac(ts_[0:nt, nnorm * 128:ncol], p[0:nt, nnorm * 128:ncol], [p], [ts_])
                    else:
                        evac(ts_[0:nt, 0:ncol], p[0:nt, 0:ncol], [p], [ts_])
                    if kind == 'nq':
                        k.dma('sp', qn, qn.t[t0:t0 + nt, c0:c0 + ncol], ts_, ts_[0:nt, 0:ncol])
                    elif kind == 'small':
                        k.dma('sp', small, small.t[t0:t0 + nt, :], ts_, ts_[0:nt, 0:ncol])
                    else:
                        dst = {'cmp': o_cmp, 'slc': o_slc, 'win': o_win}[kind]
                        k.dma('sp', dst, dst.t[t0:t0 + nt, :], ts_, ts_[0:nt, :])
                        if kind == 'win' and t0 == SEQ:
                            for s in range(NS):
                                k.dma('sp', o_winS, o_winS.t[s, 508:512, :], ts_, ts_[4 * s:4 * s + 4, :])
            for s in range(NS):
                k.dma('sp', o_winS, o_winS.t[s, 0:508, :], win_cache, win_cache.t[s, 4:512, :], par=True)
        k.end_scope(scA)

    k.end_scope(scH)
    cp_s = k.sbuf('cp_s', [128, 7, 128], F32)
    k.dma('sp', cp_s, cp_s[:], c_pack, c_pack.t)
    C_ = {}
    for i_, nm_ in enumerate(('tri', 'ntri', 'ones', 'nones', 'ident', 'negS', 'negT')):
        cb = k.sbuf('c_' + nm_, [128, 128], F32)
        k.op('dve', lambda e: e.tensor_copy(out=cb[:], in_=cp_s[:, i_, :]), reads=[cp_s], writes=[cb])
        C_[nm_] = cb
    C_['ones_bf'] = ones_bf
    C_['eps'] = eps_t
    if stage >= 2:
        for hh in range(2):
            d = H[hh]
            gdn_phase(k, C_, d['featT'], d['small'], oT_all, d['o_stP'], d['o_stS'], d['state_in'], d['convc'], d['convw'], d['alog_b'], d['dtb_b'], gnorm,
                      cut=cut, nheads=nheads, ft0=16 * hh)
    if stage >= 3:
        for hh in range(2):
            d = H[hh]
            T = dict(NT)
            T.update(qn=d['qn'], small=d['small'], o_cmp=d['o_cmp'], o_slc=d['o_slc'], o_win=d['o_win'])
            nsa_phase(k, C_, hh, T, **nsa_kw)
    if stage >= 4:
        phase_c(k, C_, oT_all, NT, o_y, **c_kw)

    k.finish(outs)
    print('instructions', k.n_inst, 'waits', k.n_wait)
    return k

_SPL = dict(gq=0, gk=2048, gv=4096, gz=6144, gb=8192, ga=8208, nq=8224, ncmp=10272, nslc=11296, nwin=12320, ngate=13344)


def _slc_ov(n_cmp, n_slc):
    cs = np.arange(n_cmp) * 16
    ss = np.arange(n_slc) * 64
    lo = np.maximum(cs[:, None], ss[None, :])
    hi = np.minimum(cs[:, None] + 32, ss[None, :] + 64)
    return (np.maximum(hi - lo, 0) / 32).astype(np.float32)


def _shared(I):
    f32 = np.float32
    S = {}
    w_in = I['w_in'][0]
    for hh in range(2):
        cols = []
        for j in range(NH):
            hg = 8 * hh + j
            for nm in ('gq', 'gk', 'gv', 'gz'):
                cols.append(np.arange(_SPL[nm] + hg * 128, _SPL[nm] + hg * 128 + 128))
        S['w_fm%d' % hh] = np.ascontiguousarray(w_in[:, np.concatenate(cols)])
        cols = [np.arange(_SPL['nq'] + 1024 * hh, _SPL['nq'] + 1024 * hh + 1024)]
        for nm in ('ncmp', 'nslc', 'nwin'):
            for kv in range(2):
                cols.append(np.arange(_SPL[nm] + kv * 512 + 256 * hh, _SPL[nm] + kv * 512 + 256 * hh + 256))
        cols.append(np.arange(_SPL['gb'] + 8 * hh, _SPL['gb'] + 8 * hh + 8))
        cols.append(np.arange(_SPL['ga'] + 8 * hh, _SPL['ga'] + 8 * hh + 8))
        cols.append(np.arange(_SPL['ngate'] + 24 * hh, _SPL['ngate'] + 24 * hh + 24))
        S['w_tm%d' % hh] = np.ascontiguousarray(w_in[:, np.concatenate(cols)])
        cw = I['gdn_conv_w'][0].reshape(4, 3, 16, 128)[:, :, 8 * hh:8 * hh + 8, :]
        S['convw%d' % hh] = np.ascontiguousarray(cw.transpose(3, 2, 1, 0))
        S['alog_b%d' % hh] = np.ascontiguousarray(np.broadcast_to(I['gdn_a_log'][0, 8 * hh:8 * hh + 8][None], (128, 8)))
        S['dtb_b%d' % hh] = np.ascontiguousarray(np.broadcast_to(I['gdn_dt_bias'][0, 8 * hh:8 * hh + 8][None], (128, 8)))
    S['g_attn'] = np.ascontiguousarray(I['attn_norm_g'][0].reshape(KT, 128).T)
    S['g_mlp'] = np.ascontiguousarray(I['mlp_norm_g'][0].reshape(KT, 128).T)
    S['gq_b'] = np.ascontiguousarray(np.broadcast_to(I['q_norm_g'][0][None, :], (128, 128)))
    S['gk_b'] = np.ascontiguousarray(np.broadcast_to(I['k_norm_g'][0][None], (128, 3, 128)))
    S['gk0c'] = np.ascontiguousarray(I['k_norm_g'][0, 0].reshape(128, 1))
    S['gnorm'] = np.ascontiguousarray(I['gdn_norm_g'][0].reshape(128, 1))
    cp = np.zeros((128, 7, 128), f32)
    pi_, fi_ = np.meshgrid(np.arange(128), np.arange(128), indexing='ij')
    cp[:, 0] = (pi_ <= fi_); cp[:, 1] = -cp[:, 0]; cp[:, 2] = 1.0; cp[:, 3] = -1.0; cp[:, 4] = (pi_ == fi_)
    cp[:, 5] = np.where(pi_ > fi_, 0.0, -30000.0); cp[:, 6] = np.where(fi_ >= pi_, 0.0, -30000.0)
    S['c_pack'] = cp
    p = np.arange(128)
    t = np.arange(SEQ)
    mc = ((16 * p[:, None] + 31) <= t[None, :]).astype(f32); mc[127] = 0
    S['maskC'] = mc
    tl = np.arange(512)
    S['causD'] = np.stack([((128 * d + p[:, None]) <= tl[None, :]) for d in range(4)], 1).astype(f32)
    wd = []
    for d in range(-4, 4):
        df = tl[None, :] - (128 * d + p[:, None])
        wd.append((df >= 0) & (df < 512))
    S['winD'] = np.stack(wd, 1).astype(f32)
    ex = np.zeros((32, 16, 128), f32)
    for kt in range(16):
        ex[2 * kt, kt, :64] = 1; ex[2 * kt + 1, kt, 64:] = 1
    S['ExAll'] = ex
    ov = np.zeros((128, 32), f32); ov[:127] = _slc_ov(127, 32)
    S['ovP'] = ov
    tt_ = (np.arange(16)[None, :] * 128 + p[:, None])
    cur = tt_ // 64
    j = np.arange(32)
    forced = (j[None, None, :] == 0) | ((j[None, None, :] <= cur[..., None]) & (j[None, None, :] > cur[..., None] - 2))
    future = j[None, None, :] > cur[..., None]
    S['keepP'] = (~forced & ~future).astype(f32)
    S['biasP'] = np.where(forced, 1e9, np.where(future, -1e9, 0.0)).astype(f32)
    pw = I['cmp_pos_w'][0]
    wp = np.zeros((2, 128, 2, 2, 16), f32)
    for hh in range(2):
        for gl in range(2):
            for kv in range(2):
                for c in range(2):
                    for m in range(8):
                        wp[hh, 16 * m:16 * m + 16, gl, kv, c * 8 + m] = pw[kv, c * 16:c * 16 + 16, 2 * hh + gl]
    S['wpool'] = wp
    ph = I['cmp_phi'][0]
    S['phi'] = np.ascontiguousarray(np.stack([np.stack([np.stack([ph[kv, 2 * hh + gl] for kv in range(2)], 1) for gl in range(2)], 1) for hh in range(2)], 0))
    ovs = np.zeros((512, 129), f32); ovs[:511] = _slc_ov(511, 129)
    S['ovS'] = np.ascontiguousarray(ovs.reshape(4, 128, 129).transpose(1, 0, 2))
    ks = np.ones((32, 129), f32); bs = np.zeros((32, 129), f32)
    for jj in (0, 127, 128):
        ks[:, jj] = 0; bs[:, jj] = 1e9
    S['keepS'] = ks; S['biasS'] = bs
    exs = np.zeros((128, 64, 128), f32)
    for pg in range(64):
        exs[2 * pg, pg, :64] = 1; exs[2 * pg + 1, pg, 64:] = 1
    S['ExS'] = exs
    mws = np.zeros((128, 4, 32), f32)
    for kt in range(4):
        r_ = kt * 128 + p
        for c in range(32):
            mws[:, kt, c] = (r_ >= (c % 4) + 1)
    S['maskWS'] = mws
    mn = np.zeros((32, 8, 32), f32)
    for s in range(8):
        for jn in range(4):
            for tq in range(4):
                if jn <= tq:
                    mn[4 * s + jn, s, 4 * s + tq] = 1
    S['maskNs'] = mn
    S['iota_p'] = p.astype(f32).reshape(128, 1)
    S['cache_cmp'] = np.ascontiguousarray(I['cache_cmp_kv'][0]).reshape(2560 * 128, 1024)
    cs = I['cache_slc_kv'][0]
    S['slcKT'] = np.ascontiguousarray(cs[:, :, 0].transpose(0, 3, 2, 1)).reshape(2560 * 128, 512)
    S['slcV'] = np.ascontiguousarray(cs[:, :, 1]).reshape(2560 * 128, 512)
    rows = []
    for hh in range(2):
        for jh in range(8):
            rows.append(np.arange((8 * hh + jh) * 128, (8 * hh + jh) * 128 + 128))
        for jh in range(8):
            rows.append(np.arange(2048 + (8 * hh + jh) * 128, 2048 + (8 * hh + jh) * 128 + 128))
    S['w_o_p'] = np.ascontiguousarray(I['w_o'][0][np.concatenate(rows)])
    S['w_up'] = np.ascontiguousarray(I['w_up'][0])
    S['w_down'] = np.ascontiguousarray(I['w_down'][0])
    return S


def _prep_core(c, I, S=None):
    b, h = c // 2, c % 2
    f32 = np.float32
    if S is None:
        S = _shared(I)
    m = dict(S)
    xs = I['x_sample'][8 * b:8 * b + 8].reshape(32, D_MODEL)
    m['xT'] = np.ascontiguousarray(np.concatenate([I['x_prompt'][b], xs], axis=0).T)
    xo = np.concatenate([I['x_prompt'][b, 1024 * h:1024 * h + 1024], I['x_sample'][8 * b + 4 * h:8 * b + 4 * h + 4].reshape(16, D_MODEL)], axis=0)
    m['x_own'] = np.ascontiguousarray(xo.T)
    hs = np.zeros((128, 2), f32); hs[:, h] = 1.0
    m['hsel'] = hs
    m['page_tab'] = np.ascontiguousarray(I['page_table'][8 * b:8 * b + 8].reshape(1, NS * 64).astype(np.int32))
    wk = I['cache_win_kv'][0, 8 * b:8 * b + 8]
    m['winKT'] = np.ascontiguousarray(np.stack([wk[:, :, 0, 2 * hh:2 * hh + 2, :].transpose(0, 2, 3, 1) for hh in range(2)], 0))
    m['winV'] = np.ascontiguousarray(np.stack([wk[:, :, 1, 2 * hh:2 * hh + 2, :].transpose(0, 2, 1, 3) for hh in range(2)], 0))
    for hh in range(2):
        m['win_cache%d' % hh] = np.ascontiguousarray(wk[:, :, :, 2 * hh:2 * hh + 2, :]).reshape(NS, 512, 512)
        cc = I['cache_gdn_conv'][0, 8 * b:8 * b + 8].reshape(NS, 3, 3, 16, 128)[:, :, :, 8 * hh:8 * hh + 8, :]
        m['convc%d' % hh] = np.ascontiguousarray(cc.transpose(4, 3, 2, 0, 1))
        m['state_in%d' % hh] = np.ascontiguousarray(I['state_gdn'][0, 8 * b:8 * b + 8, 8 * hh:8 * hh + 8])
    return m


def kernel(**I):
    import os
    I = {k_: np.asarray(v) for k_, v in I.items()}
    kk = build(stage=int(os.environ.get('KSTAGE', '9')))
    S = _shared(I)
    in_maps = [_prep_core(c, I, S) for c in range(8)]
    res = run_bass_kernel_spmd(kk.nc, in_maps, core_ids=list(range(8)))
    R = res.results
    f32 = np.float32
    G, D = 4, 128
    y_p = np.zeros((4, SEQ, D_MODEL), f32)
    y_s = np.zeros((32, 4, D_MODEL), f32)
    cmp_p = np.zeros((1, 4, SEQ, 2, G, D), f32); cmp_s = np.zeros((1, 32, 4, 2, G, D), f32)
    slc_p = np.zeros_like(cmp_p); slc_s = np.zeros_like(cmp_s)
    win_p = np.zeros((1, 4, 512, 2, G, D), f32); win_s = np.zeros((1, 32, 512, 2, G, D), f32)
    conv_p = np.zeros((1, 4, 3, 6144), f32); conv_s = np.zeros((1, 32, 3, 6144), f32)
    st_p = np.zeros((1, 4, 16, D, D), f32); st_s = np.zeros((1, 32, 16, D, D), f32)
    for c in range(8):
        b, h = c // 2, c % 2
        r = R[c]
        oy = r['o_y']
        y_p[b, 1024 * h:1024 * h + 1024] = oy[:, :1024].T
        y_s[8 * b + 4 * h:8 * b + 4 * h + 4] = oy[:, 1024:1040].T.reshape(4, 4, D_MODEL)
        if h != 0:
            continue
        for hh in range(2):
            for nm, P, S_ in (('o_cmp', cmp_p, cmp_s), ('o_slc', slc_p, slc_s)):
                a = r[nm + str(hh)].reshape(TOK, 2, 2, D)
                P[0, b, :, :, 2 * hh:2 * hh + 2, :] = a[:SEQ]
                S_[0, 8 * b:8 * b + 8, :, :, 2 * hh:2 * hh + 2, :] = a[SEQ:].reshape(8, 4, 2, 2, D)
            a = r['o_win%d' % hh].reshape(TOK, 2, 2, D)
            win_p[0, b, :, :, 2 * hh:2 * hh + 2, :] = a[SEQ - 512:SEQ]
            win_s[0, 8 * b:8 * b + 8, :, :, 2 * hh:2 * hh + 2, :] = r['o_winS%d' % hh].reshape(8, 512, 2, 2, D)
            oc = r['o_conv%d' % hh].reshape(27, NH, 3, D)
            for j in range(NH):
                hg = 8 * hh + j
                for q3 in range(3):
                    conv_p[0, b, :, q3 * 2048 + hg * 128:q3 * 2048 + hg * 128 + 128] = oc[0:3, j, q3]
                    conv_s[0, 8 * b:8 * b + 8, :, q3 * 2048 + hg * 128:q3 * 2048 + hg * 128 + 128] = oc[3:27, j, q3].reshape(8, 3, D)
            st_p[0, b, 8 * hh:8 * hh + 8] = r['o_stP%d' % hh]
            st_s[0, 8 * b:8 * b + 8, 8 * hh:8 * hh + 8] = r['o_stS%d' % hh]
    return (y_p, y_s, cmp_p, cmp_s, slc_p, slc_s, win_p, win_s, conv_p, conv_s, st_p, st_s)
```
